# Optimizing a Trainium2 kernel written in Bass

```python
import math
import jax
import jax.numpy as jnp
from jax import lax
import numpy as np

D_MODEL = 1024
BATCH = 4
SEQ = 4096
DEPTH = 2

GRID_W = 64
CTX_LEN = 256
N_EVEN = (DEPTH + 1) // 2
N_ODD = DEPTH // 2
N_MOD = 6
ATT_WIDTH = D_MODEL // 2
HY_WIDTH = D_MODEL - ATT_WIDTH
ATT_V_DIM = 128
ATT_HEADS = ATT_WIDTH // ATT_V_DIM
ATT_QK_DIM = ATT_V_DIM // 2
Q_COLS = ATT_HEADS * 2 * ATT_QK_DIM
K_COLS = ATT_HEADS * 2 * ATT_QK_DIM
V_COLS = ATT_HEADS * ATT_V_DIM
KV_START = Q_COLS
HY_START = Q_COLS + K_COLS + V_COLS
HY_COLS = 3 * HY_WIDTH
IN_COLS = HY_START + HY_COLS
Q_BLOCK = 128
ROPE_BASE = 10000.0
HY_POS_EMB = 33
HY_FILTER_HIDDEN = 64
HY_SHORT_CONV = 3
HY_DECAY_TARGET = 1e-2
HY_FAST_DECAY_PCT = 0.3
HY_SLOW_DECAY_PCT = 1.5
HY_FILTER_STD = 0.005
POOL_WINDOWS = (2, 4, 8, 16)
POOL_GROUPS = len(POOL_WINDOWS)
POOL_GROUP_DIM = D_MODEL // POOL_GROUPS
D_FF = 4 * D_MODEL
NORM_EPS = 1e-6
SUBLN_EPS = 1e-5

kernel_name = "hybrid_diffattn_hyena_pool_dit_block"


def _rmsnorm(x, g, eps=NORM_EPS):
    xf = x.astype(jnp.float32)
    y = xf * lax.rsqrt(jnp.mean(xf * xf, axis=-1, keepdims=True) + eps)
    return (y * g.astype(jnp.float32)).astype(x.dtype)


def _modulate(h, shift, scale):
    return h * (1 + scale) + shift


def _grid_positions(seq_len):
    rows = seq_len // GRID_W
    t = jnp.arange(rows * GRID_W, dtype=jnp.int32)
    return t // GRID_W, t % GRID_W


def _axial_rope(x, row, col):
    axis_dim = ATT_QK_DIM // 2
    n_freq = axis_dim // 2
    inv = ROPE_BASE ** (-jnp.arange(n_freq, dtype=jnp.float32) / n_freq)

    def rot(xa, pos):
        ang = pos.astype(jnp.float32)[:, None] * inv[None, :]
        cos = jnp.cos(ang)[None, :, None, None, :].astype(x.dtype)
        sin = jnp.sin(ang)[None, :, None, None, :].astype(x.dtype)
        x1, x2 = xa[..., :n_freq], xa[..., n_freq:]
        return jnp.concatenate([x1 * cos - x2 * sin, x1 * sin + x2 * cos], axis=-1)

    return jnp.concatenate([rot(x[..., :axis_dim], row), rot(x[..., axis_dim:], col)], axis=-1)


def _split_q(p):
    b, l = p.shape[0], p.shape[1]
    return p[..., :Q_COLS].reshape(b, l, ATT_HEADS, 2, ATT_QK_DIM)


def _split_kv(kv):
    b, l = kv.shape[0], kv.shape[1]
    k = kv[..., :K_COLS].reshape(b, l, ATT_HEADS, 2, ATT_QK_DIM)
    v = kv[..., K_COLS:].reshape(b, l, ATT_HEADS, ATT_V_DIM)
    return k, v


def _diff_attn(q, k, v, lam):
    s = jnp.einsum("bqhcd,bkhcd->bhcqk", q, k, preferred_element_type=jnp.float32) * (ATT_QK_DIM ** -0.5)
    p = jax.nn.softmax(s, axis=-1)
    w = p[:, :, 0] - lam * p[:, :, 1]
    return jnp.einsum("bhqk,bkhd->bqhd", w.astype(v.dtype), v)


def _diff_attn_blocked(q, k, v, lam):
    b, l = q.shape[0], q.shape[1]
    nb = l // Q_BLOCK
    qb = jnp.moveaxis(q.reshape(b, nb, Q_BLOCK, ATT_HEADS, 2, ATT_QK_DIM), 1, 0)
    out = lax.map(lambda qq: _diff_attn(qq, k, v, lam), qb)
    return jnp.moveaxis(out, 0, 1).reshape(b, l, ATT_HEADS, ATT_V_DIM)


def _diff_post(o, g, lam_init):
    o = _rmsnorm(o, g, SUBLN_EPS) * (1.0 - lam_init)
    return o.reshape(o.shape[0], o.shape[1], ATT_HEADS * ATT_V_DIM)


def _short_conv3(u, w, b):
    up = jnp.pad(u, ((0, 0), (1, 1), (0, 0)))
    return up[:, :-2] * w[0] + up[:, 1:-1] * w[1] + up[:, 2:] * w[2] + b


def _hyena_filter(seq_len, w1, b1, f1, w2, b2, f2, w3):
    t = jnp.linspace(0.0, 1.0, seq_len, dtype=jnp.float32)[:, None]
    bands = (HY_POS_EMB - 1) // 2
    freqs = jnp.linspace(1e-4, bands - 1, bands, dtype=jnp.float32)
    ang = (2.0 * math.pi / seq_len) * jnp.arange(seq_len, dtype=jnp.float32)[:, None] * freqs[None, :]
    z = jnp.concatenate([t, jnp.cos(ang), -jnp.sin(ang)], axis=-1).astype(w1.dtype)
    hdn = jnp.sin(f1 * (z @ w1 + b1))
    hdn = jnp.sin(f2 * (hdn @ w2 + b2))
    h = (hdn @ w3).astype(jnp.float32).reshape(seq_len, 2, HY_WIDTH)
    max_decay = math.log(HY_DECAY_TARGET) / HY_FAST_DECAY_PCT
    min_decay = math.log(HY_DECAY_TARGET) / HY_SLOW_DECAY_PCT
    deltas = jnp.linspace(min_decay, max_decay, HY_WIDTH, dtype=jnp.float32)
    h = h * jnp.exp(-t * jnp.abs(deltas)[None, :])[:, None, :]
    return jnp.concatenate([h[:, 0], jnp.zeros((1, HY_WIDTH), jnp.float32), h[:0:-1, 1]], axis=0)


def _long_conv(u, k2, bias):
    l = u.shape[1]
    uf = u.astype(jnp.float32)
    U = jnp.fft.rfft(uf, n=2 * l, axis=1)
    K = jnp.fft.rfft(k2, axis=0)
    y = jnp.fft.irfft(U * K[None], n=2 * l, axis=1)[:, :l]
    return (y + uf * bias.astype(jnp.float32)).astype(u.dtype)


def _hyena(p, conv_w, conv_b, filt, bias):
    p = _short_conv3(p, conv_w, conv_b)
    x0, x1, v = jnp.split(p, 3, axis=-1)
    return x0 * _long_conv(v * x1, filt, bias)


def _multiscale_pool(u, w, scale):
    b, l, d = u.shape
    uf = u.astype(jnp.float32)
    cs = jnp.concatenate([jnp.zeros((b, 1, d), jnp.float32), jnp.cumsum(uf, axis=1)], axis=1)
    t = jnp.arange(l)
    outs = []
    for g, win in enumerate(POOL_WINDOWS):
        lo = jnp.clip(t - win // 2, 0, l)
        hi = jnp.clip(t + win - win // 2, 0, l)
        sl = slice(g * POOL_GROUP_DIM, (g + 1) * POOL_GROUP_DIM)
        csg = cs[..., sl]
        mean = (csg[:, hi] - csg[:, lo]) / (hi - lo).astype(jnp.float32)[None, :, None]
        outs.append(mean - uf[..., sl])
    dlt = jnp.stack(outs, axis=2).astype(u.dtype)
    y = jnp.einsum("blgc,gcd->blgd", dlt, w).reshape(b, l, d)
    return y * scale


def _sq_relu_mlp(h, w1, w2):
    return jnp.square(jax.nn.relu(h @ w1)) @ w2


def setup_inputs(seed: int = 0) -> dict:
    key = jax.random.key(seed)
    ks = iter(jax.random.split(key, 40))

    def nrm(shape, std):
        return std * jax.random.normal(next(ks), shape, jnp.float32)

    D = D_MODEL
    H = HY_FILTER_HIDDEN
    return {
        "x": nrm((BATCH, SEQ, D), 1.0),
        "c": nrm((BATCH, D), 1.0),
        "ctx": nrm((BATCH, CTX_LEN, D), 1.0),
        "c_ctx": nrm((D,), 1.0),
        "ada_w": nrm((DEPTH, D, N_MOD * D), 0.5 * D ** -0.5),
        "ada_b": nrm((DEPTH, N_MOD * D), 0.02),
        "norm1_g": 1.0 + nrm((DEPTH, D), 0.05),
        "norm2_g": 1.0 + nrm((DEPTH, D), 0.05),
        "mix_w_in": nrm((N_EVEN, D, IN_COLS), D ** -0.5),
        "mix_b_in": nrm((N_EVEN, IN_COLS), 0.02),
        "mix_w_out": nrm((N_EVEN, D, D), D ** -0.5),
        "mix_b_out": nrm((N_EVEN, D), 0.02),
        "lam_q1": nrm((N_EVEN, ATT_QK_DIM), 0.1),
        "lam_k1": nrm((N_EVEN, ATT_QK_DIM), 0.1),
        "lam_q2": nrm((N_EVEN, ATT_QK_DIM), 0.1),
        "lam_k2": nrm((N_EVEN, ATT_QK_DIM), 0.1),
        "subln_g": 1.0 + nrm((N_EVEN, ATT_V_DIM), 0.05),
        "hy_conv_w": nrm((N_EVEN, HY_SHORT_CONV, HY_COLS), HY_SHORT_CONV ** -0.5),
        "hy_conv_b": nrm((N_EVEN, HY_COLS), 0.02),
        "hy_pos_w1": nrm((N_EVEN, HY_POS_EMB, H), HY_POS_EMB ** -0.5),
        "hy_pos_b1": nrm((N_EVEN, H), 0.1),
        "hy_freq1": 1.0 + nrm((N_EVEN, H), 0.1),
        "hy_pos_w2": nrm((N_EVEN, H, H), H ** -0.5),
        "hy_pos_b2": nrm((N_EVEN, H), 0.1),
        "hy_freq2": 1.0 + nrm((N_EVEN, H), 0.1),
        "hy_pos_w3": nrm((N_EVEN, H, 2 * HY_WIDTH), HY_FILTER_STD),
        "hy_bias": nrm((N_EVEN, HY_WIDTH), 0.5),
        "pool_w": nrm((N_ODD, POOL_GROUPS, POOL_GROUP_DIM, POOL_GROUP_DIM), POOL_GROUP_DIM ** -0.5),
        "pool_scale": 1.0 + nrm((N_ODD, D), 0.05),
        "mlp_w1": nrm((DEPTH, D, D_FF), D ** -0.5),
        "mlp_w2": nrm((DEPTH, D_FF, D), D_FF ** -0.5),
        "final_g": 1.0 + nrm((D,), 0.05),
    }


def reference(x, c, ctx, c_ctx, ada_w, ada_b, norm1_g, norm2_g, mix_w_in, mix_b_in, mix_w_out, mix_b_out,
              lam_q1, lam_k1, lam_q2, lam_k2, subln_g, hy_conv_w, hy_conv_b, hy_pos_w1, hy_pos_b1, hy_freq1,
              hy_pos_w2, hy_pos_b2, hy_freq2, hy_pos_w3, hy_bias, pool_w, pool_scale, mlp_w1, mlp_w2, final_g):
    seq_len = x.shape[1]
    row, col = _grid_positions(seq_len)
    s_lat = jax.nn.silu(c)
    s_ctx = jax.nn.silu(c_ctx)
    h_lat, h_ctx = x, ctx
    for i in range(DEPTH):
        ctx_live = any(j % 2 == 0 for j in range(i + 1, DEPTH))
        mod_l = [m[:, None, :] for m in jnp.split(s_lat @ ada_w[i] + ada_b[i], N_MOD, axis=-1)]
        a_lat = _modulate(_rmsnorm(h_lat, norm1_g[i]), mod_l[0], mod_l[1])
        if i % 2 == 0 or ctx_live:
            mod_c = jnp.split(s_ctx @ ada_w[i] + ada_b[i], N_MOD, axis=-1)
            a_ctx = _modulate(_rmsnorm(h_ctx, norm1_g[i]), mod_c[0], mod_c[1])
        if i % 2 == 0:
            e = i // 2
            lam_init = 0.8 - 0.6 * math.exp(-0.3 * i)
            lam = (jnp.exp(jnp.sum(lam_q1[e] * lam_k1[e]).astype(jnp.float32))
                   - jnp.exp(jnp.sum(lam_q2[e] * lam_k2[e]).astype(jnp.float32)) + lam_init)
            filt = (hy_pos_w1[e], hy_pos_b1[e], hy_freq1[e], hy_pos_w2[e], hy_pos_b2[e], hy_freq2[e], hy_pos_w3[e])
            p_lat = a_lat @ mix_w_in[e] + mix_b_in[e]
            q_l = _axial_rope(_split_q(p_lat), row, col)
            k_l, v_l = _split_kv(p_lat[..., KV_START:HY_START])
            k_l = _axial_rope(k_l, row, col)
            if ctx_live:
                p_ctx = a_ctx @ mix_w_in[e] + mix_b_in[e]
                kv_ctx = p_ctx[..., KV_START:HY_START]
            else:
                kv_ctx = a_ctx @ mix_w_in[e][:, KV_START:HY_START] + mix_b_in[e][KV_START:HY_START]
            k_c, v_c = _split_kv(kv_ctx)
            k_all = jnp.concatenate([k_c, k_l], axis=1)
            v_all = jnp.concatenate([v_c, v_l], axis=1)
            att_l = _diff_post(_diff_attn_blocked(q_l, k_all, v_all, lam), subln_g[e], lam_init)
            hy_l = _hyena(p_lat[..., HY_START:], hy_conv_w[e], hy_conv_b[e], _hyena_filter(seq_len, *filt), hy_bias[e])
            y_lat = jnp.concatenate([att_l, hy_l], axis=-1) @ mix_w_out[e] + mix_b_out[e]
            if ctx_live:
                att_c = _diff_post(_diff_attn(_split_q(p_ctx), k_c, v_c, lam), subln_g[e], lam_init)
                hy_c = _hyena(p_ctx[..., HY_START:], hy_conv_w[e], hy_conv_b[e],
                              _hyena_filter(h_ctx.shape[1], *filt), hy_bias[e])
                y_ctx = jnp.concatenate([att_c, hy_c], axis=-1) @ mix_w_out[e] + mix_b_out[e]
        else:
            o = i // 2
            y_lat = _multiscale_pool(a_lat, pool_w[o], pool_scale[o])
            if ctx_live:
                y_ctx = _multiscale_pool(a_ctx, pool_w[o], pool_scale[o])
        h_lat = h_lat + mod_l[2] * y_lat
        h_lat = h_lat + mod_l[5] * _sq_relu_mlp(
            _modulate(_rmsnorm(h_lat, norm2_g[i]), mod_l[3], mod_l[4]), mlp_w1[i], mlp_w2[i])
        if ctx_live:
            h_ctx = h_ctx + mod_c[2] * y_ctx
            h_ctx = h_ctx + mod_c[5] * _sq_relu_mlp(
                _modulate(_rmsnorm(h_ctx, norm2_g[i]), mod_c[3], mod_c[4]), mlp_w1[i], mlp_w2[i])
    return _rmsnorm(h_lat, final_g)
```

```python
import math
from contextlib import ExitStack
import numpy as np
import concourse.bass as bass
import concourse.mybir as mybir
from concourse.bass_utils import run_bass_kernel_spmd

F32 = mybir.dt.float32
BF16 = mybir.dt.bfloat16
AF = mybir.ActivationFunctionType
ALU = mybir.AluOpType

D = 1024
L = 4096
NCTX = 256
NK = L + NCTX
NM = 2064
NMX = 2066
NF = 8192
DBG = False
import os
KSTOP = os.environ.get('KSTOP')

ENGS = ("pe", "act", "dve", "pool", "sp")
N_DMA_SLOTS = 8


class Prog:
    def __init__(self, nc, es):
        self.nc = nc
        self.ops = {e: [] for e in ENGS}
        self.cnt = {e: 0 for e in ENGS}
        self.pending = {e: False for e in ENGS}
        self.sem = {}
        for e in ENGS:
            self.sem[("E", e)] = es.enter_context(nc.semaphore("s_" + e))
        self.dma_q = ("sp", "act", "pool")
        self.slot_cnt = {}
        self.slot_rr = {q: 0 for q in self.dma_q}
        for q in self.dma_q:
            for s in range(N_DMA_SLOTS):
                self.sem[("D", q, s)] = es.enter_context(nc.semaphore("d_%s%d" % (q, s)))
                self.slot_cnt[(q, s)] = 0
        self.last_w = {}
        self.readers = {}
        self.waited = {e: {} for e in ENGS}
        self.rr = 0

    def _deps(self, reads, writes):
        need = {}
        def req(tok):
            if tok is None:
                return
            k, v = tok
            if need.get(k, 0) < v:
                need[k] = v
        for r in reads:
            req(self.last_w.get(r))
        for w in writes:
            req(self.last_w.get(w))
            for t in self.readers.get(w, ()):
                req(t)
        return need

    def _emit_waits(self, eng, need):
        wl = []
        for k, v in need.items():
            if self.waited[eng].get(k, 0) >= v:
                continue
            self.waited[eng][k] = v
            wl.append((k, v))
        return wl

    def _mark(self, tok, reads, writes):
        for r in reads:
            lst = self.readers.setdefault(r, [])
            lst.append(tok)
            if len(lst) > 64:
                best = {}
                for k, v in lst:
                    if best.get(k, 0) < v:
                        best[k] = v
                self.readers[r] = list(best.items())
        for w in writes:
            self.last_w[w] = tok
            self.readers[w] = []

    def op(self, eng, fn, reads=(), writes=(), signal=True):
        need = self._deps(reads, writes)
        tokval = self.cnt[eng] + 1
        key = ("E", eng)
        if eng == "pe":
            need.pop(key, None)
        if need.get(key, 0) >= tokval:
            raise RuntimeError("same-engine dep on unsignalled op (%s)" % eng)
        wl = self._emit_waits(eng, need)
        self.ops[eng].append((wl, fn, (key, 1) if signal else None))
        self._mark((key, tokval), reads, writes)
        if signal:
            self.cnt[eng] = tokval
            self.pending[eng] = False
        else:
            self.pending[eng] = True

    def dma(self, q, out, in_, reads=(), writes=()):
        need = self._deps(reads, writes)
        s = self.slot_rr[q]
        self.slot_rr[q] = (s + 1) % N_DMA_SLOTS
        k = self.slot_cnt[(q, s)]
        key = ("D", q, s)
        if k > 0 and need.get(key, 0) < 16 * k:
            need[key] = 16 * k
        wl = self._emit_waits(q, need)
        self.slot_cnt[(q, s)] = k + 1
        self.ops[q].append((wl, lambda e: e.dma_start(out=out, in_=in_), (key, 16)))
        self._mark((key, 16 * (k + 1)), reads, writes)

    def barrier(self):
        need = {}
        for e in ENGS:
            if self.pending[e]:
                raise RuntimeError("barrier with pending unsignalled ops on " + e)
            if self.cnt[e] > 0:
                need[("E", e)] = self.cnt[e]
        for (q, s), k in self.slot_cnt.items():
            if k > 0:
                need[("D", q, s)] = 16 * k
        for e in ENGS:
            wl = self._emit_waits(e, dict(need))
            if wl:
                self.ops[e].append((wl, None, None))
        self.last_w = {}
        self.readers = {}

    def emit(self):
        nc = self.nc
        with nc.Block() as block:
            def run(eng_name):
                def body(eng):
                    for wl, fn, inc in self.ops[eng_name]:
                        for k, v in wl:
                            eng.wait_ge(self.sem[k], v)
                        if fn is None:
                            continue
                        ins = fn(eng)
                        if inc is not None:
                            ins.then_inc(self.sem[inc[0]], inc[1])
                return body
            block.tensor(run("pe"))
            block.scalar(run("act"))
            block.vector(run("dve"))
            block.gpsimd(run("pool"))
            block.sync(run("sp"))

    def mm(self, out, lhsT, rhs, start, stop, r, w, signal=None, tp=None):
        if signal is None:
            signal = stop
        if tp is None:
            self.op("pe", lambda e: e.matmul(out, lhsT=lhsT, rhs=rhs, start=start, stop=stop), r, w, signal)
        else:
            self.op("pe", lambda e: e.matmul(out, lhsT=lhsT, rhs=rhs, start=start, stop=stop, tile_position=tp), r, w, signal)

    def tr(self, out, in_, ident, r, w, signal=True):
        self.op("pe", lambda e: e.transpose(out, in_, ident), r, w, signal)

    def act(self, out, in_, func, r, w, bias=None, scale=None, accum=None):
        kw = {}
        if bias is not None:
            kw["bias"] = bias
        if scale is not None:
            kw["scale"] = scale
        if accum is not None:
            kw["accum_out"] = accum
        self.op("act", lambda e: e.activation(out=out, in_=in_, func=func, **kw), r, w)

    def tt(self, eng, out, a, b, op, r, w):
        self.op(eng, lambda e: e.tensor_tensor(out=out, in0=a, in1=b, op=op), r, w)

    def ts(self, eng, out, in0, s1, s2, op0, op1, r, w):
        if s2 is None:
            self.op(eng, lambda e: e.tensor_scalar(out=out, in0=in0, scalar1=s1, scalar2=None, op0=op0), r, w)
        else:
            self.op(eng, lambda e: e.tensor_scalar(out=out, in0=in0, scalar1=s1, scalar2=s2, op0=op0, op1=op1), r, w)

    def stt(self, out, in0, scalar, in1, op0, op1, r, w):
        self.op("dve", lambda e: e.scalar_tensor_tensor(out=out, in0=in0, scalar=scalar, in1=in1, op0=op0, op1=op1), r, w)

    def cp(self, eng, out, in_, r, w):
        if eng == "act":
            self.op("act", lambda e: e.copy(out=out, in_=in_), r, w)
        else:
            self.op(eng, lambda e: e.tensor_copy(out=out, in_=in_), r, w)

    def recip(self, out, in_, r, w):
        self.op("dve", lambda e: e.reciprocal(out=out, in_=in_), r, w)

    def memset(self, eng, ap, val, w):
        self.op(eng, lambda e: e.memset(ap, val), (), w)

    def evac(self, out, in_, r, w):
        self.rr += 1
        self.cp("act" if self.rr % 2 else "dve", out, in_, r, w)


def _fft_tables(hf):
    n1 = np.arange(64)[:, None]
    k1 = np.arange(64)[None, :]
    th = 2 * np.pi * n1 * (k1 + 0.5) / 128
    D1 = np.concatenate([np.cos(th), -np.sin(th)], axis=1)
    n2 = np.arange(64)[:, None, None]
    kk1 = np.arange(64)[None, :, None]
    k2 = np.arange(64)[None, None, :]
    psi = 2 * np.pi * (n2 * (kk1 + 0.5) / NF + n2 * k2 / 64)
    Mre, Mim = np.cos(psi), -np.sin(psi)
    F2 = np.stack([Mre, Mim, -Mim], axis=2)
    kk2 = np.arange(64)[:, None]
    nn2 = np.arange(64)[None, :]
    th2 = 2 * np.pi * kk2 * nn2 / 64
    Ga = np.concatenate([np.cos(th2), np.sin(th2)], axis=1)
    Gb = np.concatenate([-np.sin(th2), np.cos(th2)], axis=1)
    I2 = np.stack([Ga, Gb], axis=1)
    k1c = np.arange(64)[:, None, None]
    n2c = np.arange(64)[None, :, None]
    n1i = np.arange(34)[None, None, :]
    n1v = 32 * hf - 1 + n1i
    n = 64 * n1v + n2c
    phi = 2 * np.pi * n * (k1c + 0.5) / NF
    valid = ((n1v >= 0) & (n1v < 64)).astype(np.float64)
    Tre = (2.0 / NF) * np.cos(phi) * valid
    Tim = -(2.0 / NF) * np.sin(phi) * valid
    I1 = np.stack([np.broadcast_to(Tre, (64, 64, 34)), np.broadcast_to(Tim, (64, 64, 34))], axis=2)
    f = lambda a: np.ascontiguousarray(a, dtype=np.float32)
    return f(D1), f(F2.reshape(64, -1)), f(I2.reshape(64, -1)), f(I1.reshape(64, -1))


def _rope_tables(tok):
    tok = np.asarray(tok)
    row = (tok // 64).astype(np.float64)
    col = (tok % 64).astype(np.float64)
    inv = 10000.0 ** (-np.arange(16, dtype=np.float64) / 16)
    cosT = np.zeros((128, tok.size))
    sinT = np.zeros((128, tok.size))
    for p in range(128):
        d = p % 64
        blk = d // 32
        i = d % 16
        second = (d % 32) >= 16
        pos = row if blk == 0 else col
        ang = pos * inv[i]
        cosT[p] = np.cos(ang)
        sinT[p] = np.sin(ang) if second else -np.sin(ang)
    return cosT.astype(np.float32), sinT.astype(np.float32)


def _perm_matrix():
    Pm = np.zeros((128, 128), np.float32)
    for m in range(128):
        d = m % 64
        k = m + 16 if (d % 32) < 16 else m - 16
        Pm[k, m] = 1.0
    return Pm


def _hyena_consts():
    t = np.linspace(0.0, 1.0, L, dtype=np.float32)
    bands = 16
    freqs = np.linspace(1e-4, bands - 1, bands, dtype=np.float32)
    ang = (np.float32(2.0 * math.pi / L) * np.arange(L, dtype=np.float32)[:, None]) * freqs[None, :]
    z = np.concatenate([t[:, None], np.cos(ang), -np.sin(ang)], axis=-1).astype(np.float32)
    max_decay = math.log(1e-2) / 0.3
    min_decay = math.log(1e-2) / 1.5
    deltas = np.abs(np.linspace(min_decay, max_decay, 512, dtype=np.float32))
    negd = (-deltas).reshape(4, 128).T
    return np.ascontiguousarray(z.T), t.reshape(1, L).copy(), np.ascontiguousarray(negd, dtype=np.float32)


def _pool_consts(hf):
    m0 = 2048 * hf
    tok = np.arange(m0 - 8, m0 + 2056)
    valid = ((tok >= 0) & (tok < L)).astype(np.float32)
    maskm = np.broadcast_to(valid[None, :], (128, NM)).copy()
    corr = np.ones((4, 2048), np.float32)
    for g, win in enumerate((2, 4, 8, 16)):
        t = np.arange(m0, m0 + 2048)
        lo = np.clip(t - win // 2, 0, L)
        hi = np.clip(t + win - win // 2, 0, L)
        corr[g] = 1.0 / (hi - lo)
    ce = np.concatenate([corr[:, :8], corr[:, -8:]], axis=1)
    ce = np.broadcast_to(ce[None], (128, 4, 16)).copy().astype(np.float32)
    tokx = np.arange(m0 - 9, m0 + 2057)
    vx = ((tokx >= 0) & (tokx < L)).astype(np.float32)
    maskx = np.concatenate([vx[:9], vx[-9:]])[None, :]
    maskx = np.broadcast_to(maskx, (128, 18)).copy()
    return maskm, ce, maskx


def build_program():
    nc = bass.Bass("TRN2", target_bir_lowering=False)
    es = ExitStack()
    ins = {}

    def I(name, shape, dt=F32):
        ins[name] = nc.dram_tensor(name, list(shape), dt, kind="ExternalInput").ap()
        return ins[name]

    x_all = I("x_all", (L, D)); x_m = I("x_m", (NMX, D)); ctx_in = I("ctx", (NCTX, D))
    cvec = I("cvec", (8, 128)); cctx = I("cctx", (8, 128))
    ada_w = I("ada_w", (2, D, 6 * D)); ada_b = I("ada_b", (2, 48, 128))
    norm1_g = I("norm1_g", (16, 128)); norm2_g = I("norm2_g", (16, 128))
    w_in = I("mix_w_in", (D, 3072)); b_in = I("mix_b_in", (24, 128)); b_in_row = I("b_in_row", (1, 3072))
    w_out = I("mix_w_out", (D, D)); b_out = I("mix_b_out", (8, 128))
    lamv = I("lamv", (1, 256)); subln = I("subln_g", (1, 128))
    cw = I("hy_conv_w", (36, 128)); cb = I("hy_conv_b", (12, 128))
    pw1 = I("hy_pos_w1", (33, 64)); pvec = I("hy_pvec", (4, 64)); pw2 = I("hy_pos_w2", (64, 64)); pw3 = I("hy_pos_w3", (64, 1024))
    hyb = I("hy_bias", (4, 128))
    pool_w = I("pool_w", (4, 256, 256)); pool_s = I("pool_scale", (8, 128))
    w1 = I("mlp_w1", (2, D, 4096)); w2 = I("mlp_w2", (2, 4096, D)); fin_g = I("final_g", (8, 128))
    t_ident = I("t_ident", (128, 128)); t_perm = I("t_perm", (128, 128))
    t_cos_a = I("t_cos_a", (128, L)); t_sin_a = I("t_sin_a", (128, L))
    t_cos_m = I("t_cos_m", (128, NM)); t_sin_m = I("t_sin_m", (128, NM))
    t_zT = I("t_zT", (33, L)); t_trow = I("t_trow", (1, L)); t_negd = I("t_negd", (128, 4))
    t_D1 = I("t_D1", (64, 128)); t_F2 = I("t_F2", (64, 64 * 3 * 64)); t_I2 = I("t_I2", (64, 256)); t_I1 = I("t_I1", (64, 64 * 2 * 34))
    t_maskm = I("t_maskm", (128, NM)); t_ce = I("t_ce", (128, 64)); t_maskx = I("t_maskx", (128, 18))

    out_d = nc.dram_tensor("out", [2048, D], F32, kind="ExternalOutput").ap()
    skind = "ExternalOutput" if DBG else "Internal"
    def S(name, shape, dt):
        return nc.dram_tensor(name, list(shape), dt, kind=skind).ap()
    KT_d = S("KT_d", (4, 128, NK), BF16)
    V_d = S("V_d", (34, 128, 512), BF16)
    QT_d = S("QT_d", (4, 128, NM), BF16)
    uT_d = S("uT_d", (512, L), BF16)
    x0T_d = S("x0T_d", (512, NM), BF16)
    ubT_d = S("ubT_d", (512, NM), BF16)
    filt_d = S("filt_d", (1024, L), BF16)
    F_d = S("F_d", (2, 128, 2 * 128 * 64), BF16)
    hyT_d = S("hyT_d", (512, NM), BF16)
    dbg_h = S("dbg_h", (5, 128, 8, NM), F32) if DBG else None
    attT_d = S("attT_d", (128, 4, NM), BF16)

    P = Prog(nc, es)
    sbuf = lambda st, name, shape, dt: st.enter_context(nc.sbuf_tensor(name, list(shape), dt))
    psum = lambda st, name, shape, dt: st.enter_context(nc.psum_tensor(name, list(shape), dt))

    VT = sbuf(es, "VT", (128, 320), F32)
    ident = sbuf(es, "ident", (128, 128), F32)
    identb = sbuf(es, "identb", (128, 128), BF16)
    onesb = sbuf(es, "onesb", (128, 128), BF16)
    mod = sbuf(es, "mod", (128, 2, 48, 2), F32)
    sc = sbuf(es, "sc", (128, 64), F32)
    pv = sbuf(es, "pv", (64, 8), F32)
    PSA = psum(es, "psa", (128, 7, 512), F32)
    PS = [PSA[:, i, :] for i in range(7)]
    PSB = psum(es, "psb", (128, 1024), BF16)

    P.dma("sp", ident[:], t_ident, (), ["ident"])
    P.dma("pool", identb[:], t_ident, (), ["identb"])
    P.memset("dve", onesb[:], 1.0, ["onesb"])

    vt_cols = {}
    with ExitStack() as ph:
        stage = sbuf(ph, "stage", (128, 3, 128), F32)
        groups = [
            [("c", cvec, 8), ("cctx", cctx, 8), ("adab0", ada_b[0], 48), ("adab1", ada_b[1], 48)],
            [("n1g", norm1_g, 16), ("n2g", norm2_g, 16), ("bin", b_in, 24), ("bout", b_out, 8), ("subln", subln, 1), ("cw", cw, 36)],
            [("cb", cb, 12), ("hyb", hyb, 4), ("pools", pool_s, 8), ("fing", fin_g, 8)],
        ]
        col = 0
        for gi, g in enumerate(groups):
            r0 = 0
            for name, ap, nr in g:
                P.dma("sp", stage[r0:r0 + nr, gi, :], ap, (), ["stage%d" % gi])
                vt_cols[name] = col + r0
                r0 += nr
            P.tr(PS[0][:, 0:r0], stage[0:r0, gi, :], ident[0:r0, 0:r0], ["stage%d" % gi, "ident"], ["ps0"])
            P.cp("dve", VT[:, col:col + r0], PS[0][:, 0:r0], ["ps0"], ["VT"])
            col += r0
        P.dma("sp", stage[0:4, 0, 0:64], pvec, ["ps0"], ["stage0"])
        P.tr(PS[0][0:64, 0:4], stage[0:4, 0, 0:64], ident[0:4, 0:4], ["stage0", "ident"], ["ps0"])
        P.cp("dve", pv[:, 0:4], PS[0][0:64, 0:4], ["ps0"], ["pv"])
        P.tt("dve", pv[:, 4:5], pv[:, 0:1], pv[:, 1:2], ALU.mult, ["pv"], ["pv"])
        P.tt("dve", pv[:, 5:6], pv[:, 2:3], pv[:, 3:4], ALU.mult, ["pv"], ["pv"])

        V = lambda name, k=0: VT[:, vt_cols[name] + k: vt_cols[name] + k + 1]
        s2 = sbuf(ph, "s2", (128, 8, 2), F32)
        P.act(s2[:, :, 0], VT[:, vt_cols["c"]:vt_cols["c"] + 8], AF.Silu, ["VT"], ["s2"])
        P.act(s2[:, :, 1], VT[:, vt_cols["cctx"]:vt_cols["cctx"] + 8], AF.Silu, ["VT", "s2"], ["s2"])
        adab = [sbuf(ph, "adab%d" % i, (128, 8, 512), F32) for i in range(2)]
        modrow = sbuf(ph, "modrow", (2, 6144), F32)
        for li in range(2):
            for g in range(12):
                bsel = (li * 12 + g) % 2
                P.dma("sp", adab[bsel][:], ada_w[li].rearrange("(k p) c -> p k c", p=128)[:, :, g * 512:(g + 1) * 512], (), ["adab%d" % bsel])
                pr = PS[3 + g % 2]
                for k in range(8):
                    P.mm(pr[0:2, :], s2[:, k, :], adab[bsel][:, k, :], k == 0, k == 7, ["adab%d" % bsel, "s2"], ["pr%d" % (g % 2)])
                P.evac(modrow[:, g * 512:(g + 1) * 512], pr[0:2, :], ["pr%d" % (g % 2)], ["modrow"])
            pm = PS[1 + li]
            for q in range(48):
                P.tr(pm[:, 2 * q:2 * q + 2], modrow[0:2, q * 128:(q + 1) * 128], ident[0:2, 0:2], ["modrow", "ident"], ["pm%d" % li], signal=(q == 47))
            ab = VT[:, vt_cols["adab%d" % li]:vt_cols["adab%d" % li] + 48]
            pmv = pm[:, 0:96].rearrange("p (q t) -> p q t", t=2)
            P.tt("dve", mod[:, li, :, 0], pmv[:, :, 0], ab, ALU.add, ["pm%d" % li, "VT"], ["mod"])
            P.tt("dve", mod[:, li, :, 1], pmv[:, :, 1], ab, ALU.add, ["pm%d" % li, "VT", "mod"], ["mod"])
        SC = {}
        def scal(name, n):
            SC[name] = len(SC_alloc)
            SC_alloc.extend([0] * n)
            return sc[:, SC[name]:SC[name] + n]
        SC_alloc = []
        def g1(li): return VT[:, vt_cols["n1g"] + 8 * li: vt_cols["n1g"] + 8 * li + 8]
        def g2(li): return VT[:, vt_cols["n2g"] + 8 * li: vt_cols["n2g"] + 8 * li + 8]
        def M(li, m, who=0): return mod[:, li, 8 * m:8 * m + 8, who]
        for li in range(2):
            a = scal("A1_%d" % li, 8)
            P.stt(a, M(li, 1), 1.0, g1(li), ALU.add, ALU.mult, ["mod", "VT"], ["sc"])
            a = scal("A2_%d" % li, 8)
            P.stt(a, M(li, 4), 1.0, g2(li), ALU.add, ALU.mult, ["mod", "VT", "sc"], ["sc"])
        a = scal("A1c", 8)
        P.stt(a, M(0, 1, 1), 1.0, g1(0), ALU.add, ALU.mult, ["mod", "VT", "sc"], ["sc"])
        a = scal("GB", 8)
        P.tt("dve", a, M(0, 2), VT[:, vt_cols["bout"]:vt_cols["bout"] + 8], ALU.mult, ["mod", "VT", "sc"], ["sc"])
        a = scal("GS", 8)
        P.tt("dve", a, M(1, 2), VT[:, vt_cols["pools"]:vt_cols["pools"] + 8], ALU.mult, ["mod", "VT", "sc"], ["sc"])
        a = scal("g08", 1)
        P.ts("dve", a, V("subln"), 0.8, None, ALU.mult, None, ["VT", "sc"], ["sc"])
        lt = sbuf(ph, "lt", (128, 4, 64), F32)
        lj = sbuf(ph, "lj", (128, 64), F32)
        P.dma("sp", lt[:].rearrange("p a b -> p (a b)"), lamv.partition_broadcast(128), (), ["lt"])
        e12 = scal("e12", 2)
        for i in range(2):
            P.tt("dve", lj[:], lt[:, 2 * i, :], lt[:, 2 * i + 1, :], ALU.mult, ["lt"], ["lj"])
            P.act(lj[:], lj[:], AF.Copy, ["lj"], ["lj", "sc"], accum=e12[:, i:i + 1])
        P.act(e12, e12, AF.Exp, ["sc"], ["sc"])
        nl = scal("neglam", 1)
        P.tt("dve", nl, e12[:, 1:2], e12[:, 0:1], ALU.subtract, ["sc"], ["sc"])
        P.ts("dve", nl, nl, -0.2, None, ALU.add, None, ["sc"], ["sc"])
    P.barrier()
    SCv = lambda name, k=0, n=1: sc[:, SC[name] + k: SC[name] + k + n]
    MODv = lambda li, m, k, who=0: mod[:, li, 8 * m + k: 8 * m + k + 1, who]

    MT = [(0, 512), (512, 512), (1024, 512), (1536, 512), (2048, 16)]
    MTX = [(0, 512), (512, 512), (1024, 512), (1536, 512), (2048, 18)]

    def norm_tok(ph, rows_ap, ntok, Acol, Bfn, outT, okey, tag):
        xt = [sbuf(ph, "xt%s%d" % (tag, i), (128, D), F32) for i in range(2)]
        xn = [sbuf(ph, "xn%s%d" % (tag, i), (128, D), F32) for i in range(2)]
        st = sbuf(ph, "nst" + tag, (128, 8), F32)
        nch = (ntok + 127) // 128

        def front(r):
            r0 = r * 128
            nr = min(128, ntok - r0)
            b = r % 2
            P.dma("sp", xt[b][0:nr, :], rows_ap[r0:r0 + nr, :], (), ["xt%s%d" % (tag, b)])
            P.act(xn[b][0:nr, :], xt[b][0:nr, :], AF.Square, ["xt%s%d" % (tag, b)], ["xn%s%d" % (tag, b), "nst" + tag], accum=st[0:nr, 0:1])
            P.act(st[0:nr, 1:2], st[0:nr, 0:1], AF.Sqrt, ["nst" + tag], ["nst" + tag], bias=1e-6, scale=1.0 / D)
            P.recip(st[0:nr, 2:3], st[0:nr, 1:2], ["nst" + tag], ["nst" + tag])
            P.ts("dve", xn[b][0:nr, :], xt[b][0:nr, :], st[0:nr, 2:3], None, ALU.mult, None, ["xt%s%d" % (tag, b), "nst" + tag], ["xn%s%d" % (tag, b)])

        def back(r):
            r0 = r * 128
            nr = min(128, ntok - r0)
            b = r % 2
            for half in range(2):
                pt = PS[half]
                for kk in range(4):
                    k = half * 4 + kk
                    P.tr(pt[:, kk * 128:kk * 128 + nr], xn[b][0:nr, k * 128:(k + 1) * 128], ident[0:nr, 0:nr],
                         ["xn%s%d" % (tag, b), "ident"], ["ps%d" % half], signal=(kk == 3))
                for kk in range(4):
                    k = half * 4 + kk
                    if kk % 2 == 0:
                        P.ts("dve", outT[:, k, r0:r0 + nr], pt[:, kk * 128:kk * 128 + nr], Acol(k), Bfn(k), ALU.mult, ALU.add, ["ps%d" % half, "sc", "mod"], [okey])
                    else:
                        P.act(outT[:, k, r0:r0 + nr], pt[:, kk * 128:kk * 128 + nr], AF.Identity, ["ps%d" % half, "sc", "mod"], [okey], bias=Bfn(k), scale=Acol(k))

        front(0)
        for r in range(nch):
            if r + 1 < nch:
                front(r + 1)
            back(r)

    with ExitStack() as ph:
        aT_all = sbuf(ph, "aT_all", (128, 8, L), BF16)
        aT_m = sbuf(ph, "aT_m", (128, 8, NMX), BF16)
        aT_c = sbuf(ph, "aT_c", (128, 8, NCTX), BF16)
        with ExitStack() as ph0:
            norm_tok(ph0, x_all, L, lambda k: SCv("A1_0", k), lambda k: MODv(0, 0, k), aT_all, "aT_all", "a")
            norm_tok(ph0, x_m, NMX, lambda k: SCv("A1_0", k), lambda k: MODv(0, 0, k), aT_m, "aT_m", "m")
            norm_tok(ph0, ctx_in, NCTX, lambda k: SCv("A1c", k), lambda k: MODv(0, 0, k, 1), aT_c, "aT_c", "c")
        P.barrier()
        wb = [sbuf(ph, "wb%d" % i, (128, 8, 128), BF16) for i in range(4)]
        wseq = []
        for h in range(4):
            wseq += [h, 4 + h]
        for jj in range(4):
            wseq += [12 + jj, 16 + jj, 20 + jj, 16 + jj, 20 + jj, 16 + jj, 20 + jj]
        wbc = [0]
        wiss = [0]
        def _issue_w():
            if wiss[0] < len(wseq):
                i = wiss[0] % 4
                j = wseq[wiss[0]]
                wiss[0] += 1
                P.dma("pool", wb[i][:], w_in.rearrange("(k p) c -> p k c", p=128)[:, :, j * 128:(j + 1) * 128], (), ["wb%d" % i])
        def load_w(j):
            while wiss[0] < min(wbc[0] + 3, len(wseq)):
                _issue_w()
            i = wbc[0] % 4
            assert wseq[wbc[0]] == j, (wseq[wbc[0]], j)
            wbc[0] += 1
            return wb[i], "wb%d" % i
        permb = sbuf(ph, "permb", (128, 128), BF16)
        P.dma("pool", permb[:], t_perm, (), ["permb"])
        ropet = [sbuf(ph, "ropet%d" % i, (128, 2, 512), F32) for i in range(2)]
        qraw = [sbuf(ph, "qraw%d" % i, (128, 512), BF16) for i in range(2)]
        rt1 = [sbuf(ph, "rt1%d" % i, (128, 512), F32) for i in range(2)]
        rt2 = [sbuf(ph, "rt2%d" % i, (128, 512), F32) for i in range(2)]
        kst = sbuf(ph, "kst", (128, NK), BF16)
        tcnt = [0]

        def proj_rope(wt, wkey, bias_col, src, skey, c_off, tiles, cos_d, sin_d, dst, dkey, d_off):
            for (c0, w) in tiles:
                i = tcnt[0] % 2
                tcnt[0] += 1
                pq = PS[2 + i]
                P.dma("sp", ropet[i][:, 0, 0:w], cos_d[:, c0:c0 + w], (), ["ropet%d" % i])
                P.dma("sp", ropet[i][:, 1, 0:w], sin_d[:, c0:c0 + w], (), ["ropet%d" % i])
                for k in range(8):
                    P.mm(pq[:, 0:w], wt[:, k, :], src[:, k, c_off + c0:c_off + c0 + w], k == 0, k == 7, [wkey, skey], ["pq%d" % i])
                P.act(qraw[i][:, 0:w], pq[:, 0:w], AF.Identity, ["pq%d" % i, "VT"], ["qraw%d" % i], bias=bias_col)
                psw = PS[4 + i]
                P.mm(psw[:, 0:w], permb[:], qraw[i][:, 0:w], True, True, ["permb", "qraw%d" % i], ["psw%d" % i])
                P.tt("pool", rt1[i][:, 0:w], qraw[i][:, 0:w], ropet[i][:, 0, 0:w], ALU.mult, ["qraw%d" % i, "ropet%d" % i], ["rt1%d" % i])
                P.tt("dve", rt2[i][:, 0:w], psw[:, 0:w], ropet[i][:, 1, 0:w], ALU.mult, ["psw%d" % i, "ropet%d" % i], ["rt2%d" % i])
                P.tt("dve", dst[:, d_off + c0:d_off + c0 + w], rt1[i][:, 0:w], rt2[i][:, 0:w], ALU.add, ["rt1%d" % i, "rt2%d" % i], [dkey])

        for h in range(4):
            wt, wkey = load_w(h)
            proj_rope(wt, wkey, V("bin", h), aT_m, "aT_m", 1, MT, t_cos_m, t_sin_m, kst, "kst", 0)
            P.dma("sp", QT_d[h], kst[:, 0:NM], ["kst"], ["QT_d"])
            wt, wkey = load_w(4 + h)
            pq = PS[6]
            for k in range(8):
                P.mm(pq[:, 0:NCTX], wt[:, k, :], aT_c[:, k, :], k == 0, k == 7, [wkey, "aT_c"], ["pq6"])
            P.act(kst[:, 0:NCTX], pq[:, 0:NCTX], AF.Identity, ["pq6", "VT"], ["kst"], bias=V("bin", 4 + h))
            proj_rope(wt, wkey, V("bin", 4 + h), aT_all, "aT_all", 0, [(i * 512, 512) for i in range(8)], t_cos_a, t_sin_a, kst, "kst", NCTX)
            P.dma("sp", KT_d[h], kst[:], ["kst"], ["KT_d"])
        wv = sbuf(ph, "wv", (128, 8, 512), BF16)
        bv = sbuf(ph, "bv", (128, 512), F32)
        P.dma("pool", wv[:], w_in.rearrange("(k p) c -> p k c", p=128)[:, :, 1024:1536], (), ["wv"])
        P.dma("sp", bv[:], b_in_row[0:1, 1024:1536].partition_broadcast(128), (), ["bv"])
        vst = [sbuf(ph, "vst%d" % i, (128, 512), BF16) for i in range(2)]
        for kc in range(34):
            i = kc % 2
            pv_ = PS[2 + i]
            for k in range(8):
                lhs = aT_c[:, k, kc * 128:(kc + 1) * 128] if kc < 2 else aT_all[:, k, (kc - 2) * 128:(kc - 1) * 128]
                P.mm(pv_[:], lhs, wv[:, k, :], k == 0, k == 7, ["wv", "aT_c", "aT_all"], ["pq%d" % i])
            P.tt("dve", vst[i][:], pv_[:], bv[:], ALU.add, ["pq%d" % i, "bv"], ["vst%d" % i])
            P.dma("sp", V_d[kc], vst[i][:], ["vst%d" % i], ["V_d"])
        hp = [sbuf(ph, "hp%d" % i, (128, NMX), F32) for i in range(3)]
        ha = sbuf(ph, "ha", (128, NMX), F32)
        hb_ = sbuf(ph, "hb_", (128, NMX), F32)
        hx = sbuf(ph, "hx", (128, NMX), BF16)
        mx = sbuf(ph, "mx", (128, 18), F32)
        P.dma("sp", mx[:], t_maskx, (), ["mx"])

        def proj_plain(j, src, skey, pieces, dst, dkey):
            wt, wkey = load_w(j)
            for (s0, w, d0) in pieces:
                i = tcnt[0] % 2
                tcnt[0] += 1
                pq = PS[2 + i]
                for k in range(8):
                    P.mm(pq[:, 0:w], wt[:, k, :], src[:, k, s0:s0 + w], k == 0, k == 7, [wkey, skey], ["pq%d" % i])
                P.act(dst[:, d0:d0 + w], pq[:, 0:w], AF.Identity, ["pq%d" % i, "VT"], [dkey], bias=V("bin", j))

        def conv3(src, skey, n, jcol, dst, dkey, tmp, tkey):
            w0, w1_, w2_ = V("cw", jcol), V("cw", 12 + jcol), V("cw", 24 + jcol)
            P.ts("dve", tmp[:, 0:n], src[:, 1:n + 1], w1_, V("cb", jcol), ALU.mult, ALU.add, [skey, "VT"], [tkey])
            P.stt(tmp[:, 0:n], src[:, 0:n], w0, tmp[:, 0:n], ALU.mult, ALU.add, [skey, tkey, "VT"], [tkey])
            P.stt(dst[:, 0:n], src[:, 2:n + 2], w2_, tmp[:, 0:n], ALU.mult, ALU.add, [skey, tkey, "VT"], [dkey])

        mine_pieces = [(c0, w, c0) for (c0, w) in MTX]
        for jj in range(4):
            for which in range(3):
                j = 12 + 4 * which + jj
                proj_plain(j, aT_m, "aT_m", mine_pieces, hp[which], "hp%d" % which)
                P.tt("pool", hp[which][:, 0:9], hp[which][:, 0:9], mx[:, 0:9], ALU.mult, ["hp%d" % which, "mx"], ["hp%d" % which])
                P.tt("pool", hp[which][:, NMX - 9:NMX], hp[which][:, NMX - 9:NMX], mx[:, 9:18], ALU.mult, ["hp%d" % which, "mx"], ["hp%d" % which])
            conv3(hp[0], "hp0", NM, jj, hx, "hx", ha, "ha")
            P.dma("sp", x0T_d[jj * 128:(jj + 1) * 128, :], hx[:, 0:NM], ["hx"], ["x0T_d"])
            conv3(hp[1], "hp1", NM, 4 + jj, hb_, "hb_", ha, "ha")
            conv3(hp[2], "hp2", NM, 8 + jj, hp[0], "hp0", ha, "ha")
            P.stt(hx[:, 0:NM], hb_[:, 0:NM], V("hyb", jj), hp[0][:, 0:NM], ALU.mult, ALU.mult, ["hb_", "hp0", "VT"], ["hx"])
            P.dma("sp", ubT_d[jj * 128:(jj + 1) * 128, :], hx[:, 0:NM], ["hx"], ["ubT_d"])
            for hh in range(2):
                if hh == 0:
                    pieces = [(0, 512, 1), (512, 512, 513), (1024, 512, 1025), (1536, 512, 1537), (2048, 1, 2049)]
                    zcol = 0
                else:
                    pieces = [(2047, 1, 0), (2048, 512, 1), (2560, 512, 513), (3072, 512, 1025), (3584, 512, 1537)]
                    zcol = 2049
                for which in (1, 2):
                    j = 12 + 4 * which + jj
                    proj_plain(j, aT_all, "aT_all", pieces, hp[which], "hp%d" % which)
                    P.memset("pool", hp[which][:, zcol:zcol + 1], 0.0, ["hp%d" % which])
                conv3(hp[1], "hp1", 2048, 4 + jj, hb_, "hb_", ha, "ha")
                conv3(hp[2], "hp2", 2048, 8 + jj, hp[0], "hp0", ha, "ha")
                P.tt("dve", hx[:, 0:2048], hb_[:, 0:2048], hp[0][:, 0:2048], ALU.mult, ["hb_", "hp0"], ["hx"])
                P.dma("sp", uT_d[jj * 128:(jj + 1) * 128, hh * 2048:(hh + 1) * 2048], hx[:, 0:2048], ["hx"], ["uT_d"])
    P.barrier()

    if KSTOP == '1':
        P.barrier(); P.emit(); es.close(); return nc
    with ExitStack() as ph:
        attT = sbuf(ph, "attT", (128, 4, NM), BF16)
        Vs = sbuf(ph, "Vs", (128, 34, 512), BF16)
        P.dma("sp", Vs[:], V_d.rearrange("k p c -> p k c"), ["V_d"], ["Vs"])
        KTs = [sbuf(ph, "KTs%d" % i, (128, NK), BF16) for i in range(2)]
        QTs = [sbuf(ph, "QTs%d" % i, (128, NM), BF16) for i in range(2)]
        rz = sbuf(ph, "rz", (128, 2, 512), F32)
        to = sbuf(ph, "to", (128, 2, 512), F32)
        osb = sbuf(ph, "osb", (128, 512), F32)
        sqb = sbuf(ph, "sqb", (128, 512), BF16)
        rsb = sbuf(ph, "rsb", (128, 512), F32)
        ZB = [PS[6], PSB[:].bitcast(F32)]
        zD = sbuf(ph, "zD", (128, 2, 512), F32)
        zb2 = sbuf(ph, "zb2", (128, 2, 512), BF16)
        pT2 = [sbuf(ph, "pTw%d" % i, (128, 2, 512), BF16) for i in range(3)]
        steps = []
        for h in range(4):
            for (q0, w) in MT:
                for kc in range(34):
                    steps.append((h, q0, w, kc))

        def issue_S(idx):
            h, q0, w, kc = steps[idx]
            hb = h % 2
            if q0 == 0 and kc == 0:
                P.dma("sp", KTs[hb][:], KT_d[h], ["KT_d"], ["KTs%d" % hb])
                P.dma("sp", QTs[hb][:], QT_d[h], ["QT_d"], ["QTs%d" % hb])
            p_ = idx % 2
            for c in range(2):
                P.mm(PSA[:, 2 * p_ + c, 0:w], KTs[hb][c * 64:(c + 1) * 64, kc * 128:(kc + 1) * 128], QTs[hb][c * 64:(c + 1) * 64, q0:q0 + w],
                     True, True, ["KTs%d" % hb, "QTs%d" % hb], ["S%d" % p_], signal=(c == 1))

        issue_S(0)
        for idx, (h, q0, w, kc) in enumerate(steps):
            p_ = idx % 2
            pb = idx % 3
            P.act(pT2[pb][:, :, 0:w], PSA[:, 2 * p_:2 * p_ + 2, 0:w], AF.Exp, ["S%d" % p_], ["pT%d" % pb], scale=0.125)
            if idx + 1 < len(steps):
                issue_S(idx + 1)
            for c in range(2):
                P.mm(PS[4 + c][:, 0:w], Vs[:, kc, h * 128:(h + 1) * 128], pT2[pb][:, c, 0:w], kc == 0, kc == 33, ["Vs", "pT%d" % pb], ["O%d" % c], signal=(c == 1))
            if kc % 3 == 0:
                if kc == 0:
                    P.cp("dve", zD[:, :, 0:w], pT2[pb][:, :, 0:w], ["pT%d" % pb], ["zD"])
                else:
                    P.tt("dve", zD[:, :, 0:w], zD[:, :, 0:w], pT2[pb][:, :, 0:w], ALU.add, ["pT%d" % pb, "zD"], ["zD"])
            else:
                for c in range(2):
                    P.mm(ZB[c][:, 0:w], onesb[:], pT2[pb][:, c, 0:w], kc == 1, False, ["onesb", "pT%d" % pb], ["Z%d" % c], signal=(c == 1))
            if kc == 33:
                P.cp("dve", zb2[:, :, 0:w], zD[:, :, 0:w], ["zD"], ["zb"])
                for c in range(2):
                    P.mm(ZB[c][:, 0:w], onesb[:], zb2[:, c, 0:w], False, True, ["onesb", "zb"], ["Z%d" % c], signal=(c == 1))
                for c in range(2):
                    P.recip(rz[:, c, 0:w], ZB[c][:, 0:w], ["Z%d" % c], ["rz%d" % c])
                    P.tt("dve", to[:, c, 0:w], PS[4 + c][:, 0:w], rz[:, c, 0:w], ALU.mult, ["O%d" % c, "rz%d" % c], ["to%d" % c])
                P.stt(osb[:, 0:w], to[:, 1, 0:w], SCv("neglam"), to[:, 0, 0:w], ALU.mult, ALU.add, ["to0", "to1", "sc"], ["osb"])
                P.act(sqb[:, 0:w], osb[:, 0:w], AF.Square, ["osb"], ["sqb"])
                P.mm(PS[6][:, 0:w], onesb[:], sqb[:, 0:w], True, True, ["onesb", "sqb"], ["Z0"])
                P.act(rsb[:, 0:w], PS[6][:, 0:w], AF.Sqrt, ["Z0"], ["rsb"], bias=1e-5, scale=1.0 / 128)
                P.recip(rsb[:, 0:w], rsb[:, 0:w], ["rsb"], ["rsb"])
                P.stt(attT[:, h, q0:q0 + w], osb[:, 0:w], SCv("g08"), rsb[:, 0:w], ALU.mult, ALU.mult, ["osb", "rsb", "sc"], ["attT"])
        P.dma("sp", attT_d, attT[:], ["attT"], ["attT_d"])
    P.barrier()

    if KSTOP == '2':
        P.barrier(); P.emit(); es.close(); return nc
    with ExitStack() as ph:
        zT = sbuf(ph, "zT", (33, L), F32)
        P.dma("sp", zT[:], t_zT, (), ["zT"])
        trow = sbuf(ph, "trow", (128, L), F32)
        P.dma("sp", trow[:], t_trow.partition_broadcast(128), (), ["trow"])
        negd = sbuf(ph, "negd", (128, 4), F32)
        P.dma("sp", negd[:], t_negd, (), ["negd"])
        w1s = sbuf(ph, "w1s", (33, 64), F32)
        w2s = sbuf(ph, "w2s", (64, 64), F32)
        w3s = sbuf(ph, "w3s", (64, 1024), F32)
        P.dma("sp", w1s[:], pw1, (), ["w1s"])
        P.dma("sp", w2s[:], pw2, (), ["w2s"])
        P.dma("sp", w3s[:], pw3, (), ["w3s"])
        hd = [sbuf(ph, "hd%d" % i, (64, L), F32) for i in range(2)]
        r1 = sbuf(ph, "r1", (64, L), F32)
        r2 = sbuf(ph, "r2", (64, L), F32)
        PI = float(np.pi)

        def sin_layer(wt, wkey, kdim, src, skey, dst, dkey, fcol, fbcol):
            for t in range(8):
                pq = PS[t % 2]
                P.mm(pq[0:64, :], wt[0:kdim, :], src[0:kdim, t * 512:(t + 1) * 512], True, True, [wkey, skey], ["fp%d" % (t % 2)])
                P.act(dst[:, t * 512:(t + 1) * 512], pq[0:64, :], AF.Identity, ["fp%d" % (t % 2), "pv"], [dkey], bias=pv[:, fbcol:fbcol + 1], scale=pv[:, fcol:fcol + 1])
            P.ts("dve", r1[:], dst[:], PI, -2 * PI, ALU.is_gt, ALU.mult, [dkey], ["r1"])
            P.ts("dve", r2[:], dst[:], -PI, 2 * PI, ALU.is_lt, ALU.mult, [dkey], ["r2"])
            P.tt("dve", r1[:], r1[:], r2[:], ALU.add, ["r1", "r2"], ["r1"])
            P.tt("dve", dst[:], dst[:], r1[:], ALU.add, [dkey, "r1"], [dkey])
            P.ts("dve", dst[:], dst[:], 3.1415925, -3.1415925, ALU.min, ALU.max, [dkey], [dkey])
            P.act(dst[:], dst[:], AF.Sin, [dkey], [dkey])

        sin_layer(w1s, "w1s", 33, zT, "zT", hd[0], "hd0", 1, 4)
        sin_layer(w2s, "w2s", 64, hd[0], "hd0", hd[1], "hd1", 3, 5)
        dec = sbuf(ph, "dec", (128, L), F32)
        fst = [sbuf(ph, "fst%d" % i, (128, L), BF16) for i in range(2)]
        for cc in range(4):
            P.act(dec[:], trow[:], AF.Exp, ["trow", "negd"], ["dec"], scale=negd[:, cc:cc + 1])
            for dr in range(2):
                jcol = dr * 4 + cc
                fb = jcol % 2
                for t in range(8):
                    pq = PS[2 + t % 2]
                    P.mm(pq[:], w3s[:, jcol * 128:(jcol + 1) * 128], hd[1][:, t * 512:(t + 1) * 512], True, True, ["w3s", "hd1"], ["fq%d" % (t % 2)])
                    P.tt("dve", fst[fb][:, t * 512:(t + 1) * 512], pq[:], dec[:, t * 512:(t + 1) * 512], ALU.mult, ["fq%d" % (t % 2), "dec"], ["fst%d" % fb])
                if dr == 1:
                    P.memset("dve", fst[fb][:, 0:1], 0.0, ["fst%d" % fb])
                P.dma("sp", filt_d[dr * 512 + cc * 128: dr * 512 + (cc + 1) * 128, :], fst[fb][:], ["fst%d" % fb], ["filt_d"])
    P.barrier()

    if KSTOP == '3a':
        P.barrier(); P.emit(); es.close(); return nc
    with ExitStack() as ph:
        D1b = sbuf(ph, "D1b", (128, 128), BF16)
        F2b = sbuf(ph, "F2b", (128, 64, 3, 64), BF16)
        I2b = sbuf(ph, "I2b", (128, 2, 128), BF16)
        I1b = sbuf(ph, "I1b", (128, 64, 2, 34), BF16)
        for h in range(2):
            hs = slice(64 * h, 64 * h + 64)
            P.dma("pool", D1b[hs, :], t_D1, (), ["D1b"])
            P.dma("pool", F2b[hs].rearrange("p a b c -> p (a b c)"), t_F2, (), ["F2b"])
            P.dma("pool", I2b[hs].rearrange("p a b -> p (a b)"), t_I2, (), ["I2b"])
            P.dma("pool", I1b[hs].rearrange("p a b c -> p (a b c)"), t_I1, (), ["I1b"])
        xsb = [sbuf(ph, "xs%d" % i, (128, 128, 64), BF16) for i in range(2)]
        Ab = sbuf(ph, "Ab", (128, 2, 64, 128), BF16)
        Xb = sbuf(ph, "Xb", (128, 2, 64, 64), BF16)
        Fb = sbuf(ph, "Fb", (128, 2, 128, 64), BF16)
        Yb = sbuf(ph, "Yb", (128, 2, 64, 64), BF16)
        t1 = sbuf(ph, "t1", (128, 2048), F32)
        t2 = sbuf(ph, "t2", (128, 2048), F32)
        yT = sbuf(ph, "yT", (128, 34, 64), F32)
        x0ts = [sbuf(ph, "x0t%d" % i, (128, NM), BF16) for i in range(2)]
        ubts = [sbuf(ph, "ubt%d" % i, (128, NM), BF16) for i in range(2)]
        scnt = [0]
        HS = [slice(0, 64), slice(64, 128)]
        TP = [None, (64, 64)]

        lcnt = [0]

        def load_xs(src_fn, nch):
            si = lcnt[0] % 2
            lcnt[0] += 1
            xs = xsb[si]
            for h in range(2):
                for q4 in range(nch // 32):
                    P.dma("sp", xs[HS[h], q4 * 32:(q4 + 1) * 32, :], src_fn(h)[q4 * 32:(q4 + 1) * 32, :].rearrange("c (a b) -> a c b", b=64),
                          ["uT_d", "filt_d"], ["xs%d" % si])

        def forward(src_fn, nch, mode):
            si = scnt[0] % 2
            scnt[0] += 1
            xs = xsb[si]
            xkey = "xs%d" % si
            for c0 in range(0, nch, 4):
                pa = PS[(c0 // 4) % 2]
                for i in range(4):
                    for h in range(2):
                        P.mm(pa[HS[h], i * 128:(i + 1) * 128], xs[HS[h], c0 + i, :], D1b[HS[h], :], True, True, [xkey, "D1b"], ["pa%d" % ((c0 // 4) % 2)],
                             signal=(i == 3 and h == 1), tp=TP[h])
                P.evac(Ab[:, :, :, c0:c0 + 4].rearrange("p r k c -> p (r k) c"), pa.rearrange("p (i m) -> p m i", i=4),
                       ["pa%d" % ((c0 // 4) % 2)], ["Ab"])
            KG = 512 // (2 * nch)
            for kg in range(64 // KG):
                px = PS[2 + kg % 2]
                pxv = px.rearrange("p (k r c) -> p k r c", k=KG, r=2)
                for i in range(KG):
                    k1 = KG * kg + i
                    last = (i == KG - 1)
                    for (ro, tb, ai, st_, sp_) in ((0, 0, 0, True, False), (0, 2, 1, False, True), (1, 1, 0, True, False), (1, 0, 1, False, True)):
                        for h in range(2):
                            P.mm(pxv[HS[h], i, ro, :], F2b[HS[h], k1, tb, :], Ab[HS[h], ai, k1, 0:nch], st_, sp_, ["F2b", "Ab"], ["px%d" % (kg % 2)],
                                 signal=(last and ro == 1 and ai == 1 and h == 1), tp=TP[h])
                pin = px.rearrange("p (k r c) -> p r c k", k=KG, r=2)
                ks = slice(KG * kg, KG * kg + KG)
                if mode == "hf":
                    P.evac(Fb[:, :, 0:nch, ks], pin, ["px%d" % (kg % 2)], ["Fb"])
                elif mode == "u":
                    P.evac(Xb[:, :, 0:nch, ks], pin, ["px%d" % (kg % 2)], ["Xb"])
                else:
                    P.tt("dve", Fb[:, 0, 0:nch, ks], pin[:, 0], Fb[:, 0, 0:nch, ks], ALU.add, ["px%d" % (kg % 2), "Fb"], ["Fb"])
                    P.stt(Fb[:, 1, 0:nch, ks], pin[:, 1], -1.0, Fb[:, 1, 0:nch, ks], ALU.mult, ALU.add, ["px%d" % (kg % 2), "Fb"], ["Fb"])

        fsrc = []
        for dp in range(2):
            fsrc.append((lambda h, dp=dp: filt_d[dp * 256 + h * 128: dp * 256 + (h + 1) * 128, :], 128, "hf"))
            fsrc.append((lambda h, dp=dp: filt_d[512 + dp * 256 + h * 128: 512 + dp * 256 + (h + 1) * 128, :], 128, "hb"))
        usrc = [(lambda h, jj=jj: uT_d[jj * 128 + h * 64: jj * 128 + (h + 1) * 64, :], 64, "u") for jj in range(4)]
        allsrc = fsrc + usrc
        load_xs(allsrc[0][0], allsrc[0][1])
        for pi_ in range(4):
            load_xs(allsrc[pi_ + 1][0], allsrc[pi_ + 1][1])
            forward(*allsrc[pi_])
            if pi_ % 2 == 1:
                P.dma("sp", F_d[pi_ // 2], Fb[:].rearrange("p r c k -> p (r c k)"), ["Fb"], ["F_d"])

        def load_u(jj):
            fo = 64 * (jj % 2)
            Fsrc = F_d[jj // 2].rearrange("p (r c k) -> p r c k", r=2, c=128)
            for h in range(2):
                P.dma("sp", Fb[HS[h], :, fo:fo + 64, :], Fsrc[64 * (jj % 2):64 * (jj % 2) + 64, :, 64 * h:64 * h + 64, :], ["F_d", "Fb"], ["Fv%d" % (jj % 2)])
            P.dma("sp", x0ts[jj % 2][:], x0T_d[jj * 128:(jj + 1) * 128, :], ["x0T_d"], ["x0t%d" % (jj % 2)])
            P.dma("sp", ubts[jj % 2][:], ubT_d[jj * 128:(jj + 1) * 128, :], ["ubT_d"], ["ubt%d" % (jj % 2)])

        load_u(0)
        for jj in range(4):
            fo = 64 * (jj % 2)
            Fv = Fb[:, :, fo:fo + 64, :]
            fkey = "Fv%d" % (jj % 2)
            x0t, ubt = x0ts[jj % 2], ubts[jj % 2]
            xkey_, ukey_ = "x0t%d" % (jj % 2), "ubt%d" % (jj % 2)
            if jj + 1 < 4:
                load_xs(allsrc[4 + jj + 1][0], 64)
                load_u(jj + 1)
            forward(*allsrc[4 + jj])
            Xf = Xb[:].rearrange("p r c k -> p r (c k)")
            Ff = Fv.rearrange("p r c k -> p r (c k)")
            Yf = Yb[:].rearrange("p r c k -> p r (c k)")
            for hv in range(2):
                sl = slice(hv * 2048, (hv + 1) * 2048)
                P.tt("dve", t1[:], Xf[:, 0, sl], Ff[:, 0, sl], ALU.mult, ["Xb", fkey], ["t1"])
                P.tt("pool", t2[:], Xf[:, 1, sl], Ff[:, 1, sl], ALU.mult, ["Xb", fkey], ["t2"])
                P.tt("dve", Yf[:, 0, sl], t1[:], t2[:], ALU.subtract, ["t1", "t2"], ["Yb"])
                P.tt("dve", t1[:], Xf[:, 0, sl], Ff[:, 1, sl], ALU.mult, ["Xb", fkey], ["t1"])
                P.tt("pool", t2[:], Xf[:, 1, sl], Ff[:, 0, sl], ALU.mult, ["Xb", fkey], ["t2"])
                P.tt("dve", Yf[:, 1, sl], t1[:], t2[:], ALU.add, ["t1", "t2"], ["Yb"])
            for c0 in range(0, 64, 4):
                pz = PS[(c0 // 4) % 2]
                for i in range(4):
                    for r_ in range(2):
                        for h in range(2):
                            P.mm(pz[HS[h], i * 128:(i + 1) * 128], Yb[HS[h], r_, c0 + i, :], I2b[HS[h], r_, :], r_ == 0, r_ == 1, ["Yb", "I2b"],
                                 ["pa%d" % ((c0 // 4) % 2)], signal=(i == 3 and r_ == 1 and h == 1), tp=TP[h])
                P.evac(Ab[:, :, :, c0:c0 + 4].rearrange("p r n c -> p (r n) c"), pz.rearrange("p (i m) -> p m i", i=4),
                       ["pa%d" % ((c0 // 4) % 2)], ["Ab"])
            for bi, (n20, nb) in enumerate([(0, 15), (15, 15), (30, 15), (45, 15), (60, 4)]):
                py = PS[4 + bi % 2]
                for i in range(nb):
                    n2 = n20 + i
                    for r_ in range(2):
                        for h in range(2):
                            P.mm(py[HS[h], i * 34:(i + 1) * 34], Ab[HS[h], r_, n2, 0:64], I1b[HS[h], n2, r_, :], r_ == 0, r_ == 1, ["Ab", "I1b"],
                                 ["py%d" % (bi % 2)], signal=(i == nb - 1 and r_ == 1 and h == 1), tp=TP[h])
                P.evac(yT[:, :, n20:n20 + nb], py[:, 0:nb * 34].rearrange("p (i j) -> p j i", j=34), ["py%d" % (bi % 2)], ["yT"])
            yflat = yT[:].rearrange("p a b -> p (a b)")
            P.tt("dve", yflat[:, 56:56 + NM], yflat[:, 56:56 + NM], ubt[:], ALU.add, ["yT", ukey_], ["yT"])
            P.tt("dve", x0t[:], yflat[:, 56:56 + NM], x0t[:], ALU.mult, ["yT", xkey_], [xkey_])
            P.dma("sp", hyT_d[jj * 128:(jj + 1) * 128, :], x0t[:], [xkey_], ["hyT_d"])
    P.barrier()

    if KSTOP == '3b':
        P.barrier(); P.emit(); es.close(); return nc
    def rstd_bufs(ph, tag):
        sq = [sbuf(ph, "nsq%s%d" % (tag, i), (128, 8, 512), BF16) for i in range(2)]
        rs = [sbuf(ph, "nrs%s%d" % (tag, i), (128, 512), F32) for i in range(2)]
        return sq, rs

    def rstd_tile(bufs, ti, c_abs, w):
        sq, rs = bufs
        i = ti % 2
        P.act(sq[i][:, :, 0:w], hT[:, :, c_abs:c_abs + w], AF.Square, ["hT"], ["nsq%d" % i])
        for k in range(8):
            P.mm(PS[6][:, 0:w], onesb[:], sq[i][:, k, 0:w], k == 0, k == 7, ["onesb", "nsq%d" % i], ["nss"])
        P.act(rs[i][:, 0:w], PS[6][:, 0:w], AF.Sqrt, ["nss"], ["nrs%d" % i], bias=1e-6, scale=1.0 / D)
        P.recip(rs[i][:, 0:w], rs[i][:, 0:w], ["nrs%d" % i], ["nrs%d" % i])
        return rs[i], "nrs%d" % i

    def norm_feat(ph, col0, tiles, Acol, Bcol, outT, okey, tag):
        bufs = rstd_bufs(ph, tag)
        tmp = [sbuf(ph, "ntm%s%d" % (tag, i), (128, 512), F32) for i in range(2)]
        for ti, (c0, w) in enumerate(tiles):
            rstd, rkey = rstd_tile(bufs, ti, col0 + c0, w)
            for k in range(8):
                t_ = tmp[k % 2]
                P.stt(t_[:, 0:w], hT[:, k, col0 + c0:col0 + c0 + w], Acol(k), rstd[:, 0:w], ALU.mult, ALU.mult, ["hT", rkey, "sc", "VT"], ["ntm%d" % (k % 2)])
                P.act(outT[:, k, c0:c0 + w], t_[:, 0:w], AF.Identity, ["ntm%d" % (k % 2), "mod"], [okey], bias=Bcol(k))

    def mlp(li, col0, tiles, tag):
        with ExitStack() as ph:
            a2 = sbuf(ph, "a2" + tag, (128, 8, NM), BF16)
            w1c = [sbuf(ph, "w1c%s%d" % (tag, i), (128, 8, 8, 128), BF16) for i in range(2)]
            w2r = [sbuf(ph, "w2r%s%d" % (tag, i), (128, 8, D), BF16) for i in range(2)]
            hid = [sbuf(ph, "hid%s%d" % (tag, i), (128, 8, 512), BF16) for i in range(2)]
            rl = [sbuf(ph, "rl%s%d" % (tag, i), (128, 512), BF16) for i in range(2)]
            cnt = 0
            def wload(g):
                gb = g % 2
                for jj in range(8):
                    j = 8 * g + jj
                    P.dma("pool", w1c[gb][:, jj], w1[li].rearrange("(k p) c -> p k c", p=128)[:, :, j * 128:(j + 1) * 128], (), ["w1c%d_%d" % (gb, jj)])
                P.dma("pool", w2r[gb][:], w2[li][g * 1024:(g + 1) * 1024, :].rearrange("(j p) c -> p j c", p=128), (), ["w2r%d" % gb])
            wload(0)
            norm_feat(ph, col0, tiles, lambda k: SCv("A2_%d" % li, k), lambda k: MODv(li, 3, k), a2, "a2", tag)
            for g in range(4):
                gb = g % 2
                if g + 1 < 4:
                    wload(g + 1)
                for ti, (c0, w) in enumerate(tiles):
                    hb = ti % 2
                    for jj in range(8):
                        pb = cnt % 2
                        cnt += 1
                        ph_ = PS[pb]
                        for k in range(8):
                            P.mm(ph_[:, 0:w], w1c[gb][:, jj, k, :], a2[:, k, c0:c0 + w], k == 0, k == 7, ["w1c%d_%d" % (gb, jj), "a2"], ["mh%d" % pb])
                        P.act(rl[pb][:, 0:w], ph_[:, 0:w], AF.Relu, ["mh%d" % pb], ["rl%d" % pb])
                        P.tt("pool" if jj % 4 == 3 else "dve", hid[hb][:, jj, 0:w], rl[pb][:, 0:w], rl[pb][:, 0:w], ALU.mult, ["rl%d" % pb], ["hid%d" % hb])
                    for m in range(8):
                        po = PS[2 + m % 4]
                        for jj in range(8):
                            P.mm(po[:, 0:w], w2r[gb][:, jj, m * 128:(m + 1) * 128], hid[hb][:, jj, 0:w], jj == 0, jj == 7, ["w2r%d" % gb, "hid%d" % hb], ["mo%d" % (m % 4)])
                        P.stt(hT[:, m, col0 + c0:col0 + c0 + w], po[:, 0:w], MODv(li, 5, m), hT[:, m, col0 + c0:col0 + c0 + w], ALU.mult, ALU.add,
                              ["mo%d" % (m % 4), "mod", "hT"], ["hT"])
        P.barrier()

    hT = sbuf(es, "hT", (128, 8, NM), F32)
    with ExitStack() as ph:
        hyT = sbuf(ph, "hyT", (128, 4, NM), BF16)
        P.dma("sp", hyT[:], hyT_d.rearrange("(j p) t -> p j t", p=128), ["hyT_d"], ["hyT"])
        attT = sbuf(ph, "attT4", (128, 4, NM), BF16)
        P.dma("sp", attT[:], attT_d, ["attT_d"], ["attT"])
        xt = [sbuf(ph, "xo%d" % i, (128, D), F32) for i in range(2)]
        for r in range((NM + 127) // 128):
            r0 = r * 128
            nr = min(128, NM - r0)
            b = r % 2
            P.dma("sp", xt[b][0:nr, :], x_m[1 + r0:1 + r0 + nr, :], (), ["xo%d" % b])
            for half in range(2):
                pt = PS[half]
                for kk in range(4):
                    k = half * 4 + kk
                    P.tr(pt[:, kk * 128:kk * 128 + nr], xt[b][0:nr, k * 128:(k + 1) * 128], ident[0:nr, 0:nr], ["xo%d" % b, "ident"], ["ps%d" % half], signal=(kk == 3))
                for kk in range(4):
                    k = half * 4 + kk
                    P.act(hT[:, k, r0:r0 + nr], pt[:, kk * 128:kk * 128 + nr], AF.Identity, ["ps%d" % half, "sc"], ["hT"], bias=SCv("GB", k))
        wo = [sbuf(ph, "wo%d" % i, (128, 8, 128), BF16) for i in range(2)]
        for m in range(8):
            wi = m % 2
            P.dma("pool", wo[wi][:], w_out.rearrange("(k p) c -> p k c", p=128)[:, :, m * 128:(m + 1) * 128], (), ["wo%d" % wi])
            for ti, (c0, w) in enumerate(MT):
                po = PS[2 + ti % 2]
                for k in range(8):
                    srcT = attT[:, k, c0:c0 + w] if k < 4 else hyT[:, k - 4, c0:c0 + w]
                    P.mm(po[:, 0:w], wo[wi][:, k, :], srcT, k == 0, k == 7, ["wo%d" % wi, "attT", "hyT"], ["po%d" % (ti % 2)])
                P.stt(hT[:, m, c0:c0 + w], po[:, 0:w], MODv(0, 2, m), hT[:, m, c0:c0 + w], ALU.mult, ALU.add, ["po%d" % (ti % 2), "mod", "hT"], ["hT"])
        if DBG:
            P.dma("sp", dbg_h[0], hT[:], ["hT"], ["dbg_h"])
    P.barrier()

    mlp(0, 0, MT, "m0")
    if DBG:
        P.dma("sp", dbg_h[1], hT[:], ["hT"], ["dbg_h"])
        P.barrier()

    MAIN = [(i * 512, 512) for i in range(4)]
    with ExitStack() as ph:
        aP = sbuf(ph, "aP", (128, 8, NM), F32)
        norm_featF = None
        pbufs = rstd_bufs(ph, "p")
        for ti, (c0, w) in enumerate(MT):
            rstd, rkey = rstd_tile(pbufs, ti, c0, w)
            for k in range(8):
                P.stt(aP[:, k, c0:c0 + w], hT[:, k, c0:c0 + w], SCv("A1_1", k), rstd[:, 0:w], ALU.mult, ALU.mult, ["hT", rkey, "sc"], ["aP%d" % k])
                P.act(aP[:, k, c0:c0 + w], aP[:, k, c0:c0 + w], AF.Identity, ["aP%d" % k, "mod"], ["aP%d" % k], bias=MODv(1, 0, k))
        mk = sbuf(ph, "mk", (128, 16), F32)
        ce = sbuf(ph, "ce", (128, 4, 16), F32)
        P.dma("sp", mk[:, 0:8], t_maskm[:, 0:8], (), ["mk"])
        P.dma("sp", mk[:, 8:16], t_maskm[:, NM - 8:NM], (), ["mk"])
        P.dma("sp", ce[:].rearrange("p a b -> p (a b)"), t_ce, (), ["ce"])
        sA = sbuf(ph, "sA", (128, NM), F32)
        sB = sbuf(ph, "sB", (128, NM), F32)
        sC, sD = sA, sB
        dl = sbuf(ph, "dl", (128, 8, 2048), BF16)
        for k in range(8):
            g = k // 2
            win = (2, 4, 8, 16)[g]
            ak = aP[:, k, :]
            pe_ = "dve"
            P.tt("pool", ak[:, 0:8], ak[:, 0:8], mk[:, 0:8], ALU.mult, ["aP%d" % k, "mk"], ["aP%d" % k])
            P.tt("pool", ak[:, NM - 8:NM], ak[:, NM - 8:NM], mk[:, 8:16], ALU.mult, ["aP%d" % k, "mk"], ["aP%d" % k])
            P.tt(pe_, (sA if pe_ == "dve" else sC)[:, 1:NM], ak[:, 0:NM - 1], ak[:, 1:NM], ALU.add, ["aP%d" % k], ["sA" if pe_ == "dve" else "sC"])
            cur, ckey, oth, okey = (sA, "sA", sB, "sB") if pe_ == "dve" else (sC, "sC", sD, "sD")
            lo, hi = 1, NM
            stepw = 1
            while stepw * 2 < win:
                nlo, nhi = lo + stepw, hi - stepw
                P.tt(pe_, oth[:, nlo:nhi], cur[:, nlo - stepw:nhi - stepw], cur[:, nlo + stepw:nhi + stepw], ALU.add, [ckey], [okey])
                cur, ckey, oth, okey = oth, okey, cur, ckey
                lo, hi = nlo, nhi
                stepw *= 2
            P.ts(pe_, oth[:, 8:2056], cur[:, 8:2056], 1.0 / win, None, ALU.mult, None, [ckey], [okey])
            P.tt(pe_, oth[:, 8:16], cur[:, 8:16], ce[:, g, 0:8], ALU.mult, [ckey, "ce", okey], [okey])
            P.tt(pe_, oth[:, 2048:2056], cur[:, 2048:2056], ce[:, g, 8:16], ALU.mult, [ckey, "ce", okey], [okey])
            P.tt(pe_, dl[:, k, :], oth[:, 8:2056], ak[:, 8:2056], ALU.subtract, [okey, "aP%d" % k], ["dl"])
        pwt = [sbuf(ph, "pwt%d" % i, (128, 2, 256), BF16) for i in range(2)]
        for g in range(4):
            P.dma("pool", pwt[g % 2][:], pool_w[g].rearrange("(k p) c -> p k c", p=128), (), ["pwt%d" % (g % 2)])
            for m2 in range(2):
                m = 2 * g + m2
                for ti, (c0, w) in enumerate(MAIN):
                    po = PS[2 + ti % 2]
                    for k2 in range(2):
                        P.mm(po[:, 0:w], pwt[g % 2][:, k2, m2 * 128:(m2 + 1) * 128], dl[:, 2 * g + k2, c0:c0 + w], k2 == 0, k2 == 1, ["pwt%d" % (g % 2), "dl"], ["po%d" % (ti % 2)])
                    P.stt(hT[:, m, 8 + c0:8 + c0 + w], po[:, 0:w], SCv("GS", m), hT[:, m, 8 + c0:8 + c0 + w], ALU.mult, ALU.add, ["po%d" % (ti % 2), "sc", "hT"], ["hT"])
        if DBG:
            P.dma("sp", dbg_h[2], hT[:], ["hT"], ["dbg_h"])
    P.barrier()

    mlp(1, 8, MAIN, "m1")

    with ExitStack() as ph:
        fbufs = rstd_bufs(ph, "f")
        of = sbuf(ph, "of", (128, 8, 512), F32)
        ot = [sbuf(ph, "ot%d" % i, (128, D), F32) for i in range(2)]
        oc = 0
        for ti, (c0, w) in enumerate(MAIN):
            rstd, rkey = rstd_tile(fbufs, ti, 8 + c0, w)
            for k in range(8):
                P.stt(of[:, k, 0:w], hT[:, k, 8 + c0:8 + c0 + w], V("fing", k), rstd[:, 0:w], ALU.mult, ALU.mult, ["hT", rkey, "VT"], ["of"])
            for tcn in range(4):
                b = oc % 2
                oc += 1
                for half in range(2):
                    pt = PS[half]
                    for kk in range(4):
                        k = half * 4 + kk
                        P.tr(pt[:, kk * 128:(kk + 1) * 128], of[:, k, tcn * 128:(tcn + 1) * 128], ident[:], ["of", "ident"], ["ps%d" % half], signal=(kk == 3))
                    P.evac(ot[b][:, half * 512:(half + 1) * 512], pt[:], ["ps%d" % half], ["ot%d" % b])
                P.dma("sp", out_d[c0 + tcn * 128:c0 + (tcn + 1) * 128, :], ot[b][:], ["ot%d" % b], ["out_d"])
    P.barrier()
    P.emit()
    es.close()
    return nc


_CACHE = {}


def _core_inputs(inp, b, hf):
    f = lambda a: np.ascontiguousarray(a, dtype=np.float32)
    m0 = 2048 * hf
    xb = inp["x"][b]
    xm = np.zeros((NMX, D), np.float32)
    lo, hi = m0 - 9, m0 + 2057
    slo, shi = max(lo, 0), min(hi, L)
    xm[slo - lo:shi - lo] = xb[slo:shi]
    D1, F2, I2, I1 = _fft_tables(hf)
    cos_a, sin_a = _rope_tables(np.arange(L))
    cos_m, sin_m = _rope_tables(np.clip(np.arange(m0 - 8, m0 + 2056), 0, L - 1))
    zT, trow, negd = _hyena_consts()
    maskm, ce, maskx = _pool_consts(hf)
    d = {
        "x_all": f(xb), "x_m": xm, "ctx": f(inp["ctx"][b]),
        "cvec": f(inp["c"][b].reshape(8, 128)), "cctx": f(inp["c_ctx"].reshape(8, 128)),
        "ada_w": f(inp["ada_w"]), "ada_b": f(inp["ada_b"].reshape(2, 48, 128)),
        "norm1_g": f(inp["norm1_g"].reshape(16, 128)), "norm2_g": f(inp["norm2_g"].reshape(16, 128)),
        "mix_w_in": f(inp["mix_w_in"][0]), "mix_b_in": f(inp["mix_b_in"][0].reshape(24, 128)), "b_in_row": f(inp["mix_b_in"][0].reshape(1, 3072)),
        "mix_w_out": f(inp["mix_w_out"][0]), "mix_b_out": f(inp["mix_b_out"][0].reshape(8, 128)),
        "lamv": f(np.stack([inp["lam_q1"][0], inp["lam_k1"][0], inp["lam_q2"][0], inp["lam_k2"][0]]).reshape(1, 256)),
        "subln_g": f(inp["subln_g"].reshape(1, 128)),
        "hy_conv_w": f(inp["hy_conv_w"][0].reshape(36, 128)), "hy_conv_b": f(inp["hy_conv_b"][0].reshape(12, 128)),
        "hy_pos_w1": f(inp["hy_pos_w1"][0]),
        "hy_pvec": f(np.stack([inp["hy_pos_b1"][0], inp["hy_freq1"][0], inp["hy_pos_b2"][0], inp["hy_freq2"][0]])),
        "hy_pos_w2": f(inp["hy_pos_w2"][0]), "hy_pos_w3": f(inp["hy_pos_w3"][0]),
        "hy_bias": f(inp["hy_bias"][0].reshape(4, 128)),
        "pool_w": f(inp["pool_w"][0]), "pool_scale": f(inp["pool_scale"][0].reshape(8, 128)),
        "mlp_w1": f(inp["mlp_w1"]), "mlp_w2": f(inp["mlp_w2"]), "final_g": f(inp["final_g"].reshape(8, 128)),
        "t_ident": np.eye(128, dtype=np.float32), "t_perm": _perm_matrix(),
        "t_cos_a": cos_a, "t_sin_a": sin_a, "t_cos_m": cos_m, "t_sin_m": sin_m,
        "t_zT": zT, "t_trow": trow, "t_negd": negd,
        "t_D1": D1, "t_F2": F2, "t_I2": I2, "t_I1": I1,
        "t_maskm": maskm, "t_ce": f(ce.reshape(128, 64)), "t_maskx": maskx,
    }
    return d


def kernel(**inputs):
    inp = {k: np.asarray(v) for k, v in inputs.items()}
    if "nc" not in _CACHE:
        _CACHE["nc"] = build_program()
    nc = _CACHE["nc"]
    in_maps = []
    for core in range(8):
        b, hf = core // 2, core % 2
        in_maps.append(_core_inputs(inp, b, hf))
    res = run_bass_kernel_spmd(nc, in_maps, core_ids=list(range(8)))
    _CACHE["res"] = res
    out = np.zeros((4, L, D), np.float32)
    for core in range(8):
        b, hf = core // 2, core % 2
        out[b, 2048 * hf:2048 * (hf + 1)] = res.results[core]["out"]
    return out
```

```python
import math
from contextlib import ExitStack
import numpy as np
import concourse.bass as bass
import concourse.mybir as mybir
from concourse.bass_utils import run_bass_kernel_spmd

F32 = mybir.dt.float32
BF16 = mybir.dt.bfloat16
AF = mybir.ActivationFunctionType
ALU = mybir.AluOpType

D = 1024
L = 4096
NCTX = 256
NK = L + NCTX
NM = 2064
NMX = 2066
NF = 8192
DBG = False
import os
KSTOP = os.environ.get('KSTOP')

ENGS = ("pe", "act", "dve", "pool", "sp")
N_DMA_SLOTS = 8


class Prog:
    def __init__(self, nc, es):
        self.nc = nc
        self.ops = {e: [] for e in ENGS}
        self.cnt = {e: 0 for e in ENGS}
        self.pending = {e: False for e in ENGS}
        self.sem = {}
        for e in ENGS:
            self.sem[("E", e)] = es.enter_context(nc.semaphore("s_" + e))
        self.dma_q = ("sp", "act", "pool")
        self.slot_cnt = {}
        self.slot_rr = {q: 0 for q in self.dma_q}
        for q in self.dma_q:
            for s in range(N_DMA_SLOTS):
                self.sem[("D", q, s)] = es.enter_context(nc.semaphore("d_%s%d" % (q, s)))
                self.slot_cnt[(q, s)] = 0
        self.last_w = {}
        self.readers = {}
        self.waited = {e: {} for e in ENGS}
        self.rr = 0

    def _deps(self, reads, writes):
        need = {}
        def req(tok):
            if tok is None:
                return
            k, v = tok
            if need.get(k, 0) < v:
                need[k] = v
        for r in reads:
            req(self.last_w.get(r))
        for w in writes:
            req(self.last_w.get(w))
            for t in self.readers.get(w, ()):
                req(t)
        return need

    def _emit_waits(self, eng, need):
        wl = []
        for k, v in need.items():
            if self.waited[eng].get(k, 0) >= v:
                continue
            self.waited[eng][k] = v
            wl.append((k, v))
        return wl

    def _mark(self, tok, reads, writes):
        for r in reads:
            lst = self.readers.setdefault(r, [])
            lst.append(tok)
            if len(lst) > 64:
                best = {}
                for k, v in lst:
                    if best.get(k, 0) < v:
                        best[k] = v
                self.readers[r] = list(best.items())
        for w in writes:
            self.last_w[w] = tok
            self.readers[w] = []

    def op(self, eng, fn, reads=(), writes=(), signal=True):
        need = self._deps(reads, writes)
        tokval = self.cnt[eng] + 1
        key = ("E", eng)
        if eng == "pe":
            need.pop(key, None)
        if need.get(key, 0) >= tokval:
            raise RuntimeError("same-engine dep on unsignalled op (%s)" % eng)
        wl = self._emit_waits(eng, need)
        self.ops[eng].append((wl, fn, (key, 1) if signal else None))
        self._mark((key, tokval), reads, writes)
        if signal:
            self.cnt[eng] = tokval
            self.pending[eng] = False
        else:
            self.pending[eng] = True

    def dma(self, q, out, in_, reads=(), writes=()):
        need = self._deps(reads, writes)
        s = self.slot_rr[q]
        self.slot_rr[q] = (s + 1) % N_DMA_SLOTS
        k = self.slot_cnt[(q, s)]
        key = ("D", q, s)
        if k > 0 and need.get(key, 0) < 16 * k:
            need[key] = 16 * k
        wl = self._emit_waits(q, need)
        self.slot_cnt[(q, s)] = k + 1
        self.ops[q].append((wl, lambda e: e.dma_start(out=out, in_=in_), (key, 16)))
        self._mark((key, 16 * (k + 1)), reads, writes)

    def barrier(self):
        need = {}
        for e in ENGS:
            if self.pending[e]:
                raise RuntimeError("barrier with pending unsignalled ops on " + e)
            if self.cnt[e] > 0:
                need[("E", e)] = self.cnt[e]
        for (q, s), k in self.slot_cnt.items():
            if k > 0:
                need[("D", q, s)] = 16 * k
        for e in ENGS:
            wl = self._emit_waits(e, dict(need))
            if wl:
                self.ops[e].append((wl, None, None))
        self.last_w = {}
        self.readers = {}

    def emit(self):
        nc = self.nc
        with nc.Block() as block:
            def run(eng_name):
                def body(eng):
                    for wl, fn, inc in self.ops[eng_name]:
                        for k, v in wl:
                            eng.wait_ge(self.sem[k], v)
                        if fn is None:
                            continue
                        ins = fn(eng)
                        if inc is not None:
                            ins.then_inc(self.sem[inc[0]], inc[1])
                return body
            block.tensor(run("pe"))
            block.scalar(run("act"))
            block.vector(run("dve"))
            block.gpsimd(run("pool"))
            block.sync(run("sp"))

    def mm(self, out, lhsT, rhs, start, stop, r, w, signal=None, tp=None):
        if signal is None:
            signal = stop
        if tp is None:
            self.op("pe", lambda e: e.matmul(out, lhsT=lhsT, rhs=rhs, start=start, stop=stop), r, w, signal)
        else:
            self.op("pe", lambda e: e.matmul(out, lhsT=lhsT, rhs=rhs, start=start, stop=stop, tile_position=tp), r, w, signal)

    def tr(self, out, in_, ident, r, w, signal=True):
        self.op("pe", lambda e: e.transpose(out, in_, ident), r, w, signal)

    def act(self, out, in_, func, r, w, bias=None, scale=None, accum=None):
        kw = {}
        if bias is not None:
            kw["bias"] = bias
        if scale is not None:
            kw["scale"] = scale
        if accum is not None:
            kw["accum_out"] = accum
        self.op("act", lambda e: e.activation(out=out, in_=in_, func=func, **kw), r, w)

    def tt(self, eng, out, a, b, op, r, w):
        self.op(eng, lambda e: e.tensor_tensor(out=out, in0=a, in1=b, op=op), r, w)

    def ts(self, eng, out, in0, s1, s2, op0, op1, r, w):
        if s2 is None:
            self.op(eng, lambda e: e.tensor_scalar(out=out, in0=in0, scalar1=s1, scalar2=None, op0=op0), r, w)
        else:
            self.op(eng, lambda e: e.tensor_scalar(out=out, in0=in0, scalar1=s1, scalar2=s2, op0=op0, op1=op1), r, w)

    def stt(self, out, in0, scalar, in1, op0, op1, r, w):
        self.op("dve", lambda e: e.scalar_tensor_tensor(out=out, in0=in0, scalar=scalar, in1=in1, op0=op0, op1=op1), r, w)

    def cp(self, eng, out, in_, r, w):
        if eng == "act":
            self.op("act", lambda e: e.copy(out=out, in_=in_), r, w)
        else:
            self.op(eng, lambda e: e.tensor_copy(out=out, in_=in_), r, w)

    def recip(self, out, in_, r, w):
        self.op("dve", lambda e: e.reciprocal(out=out, in_=in_), r, w)

    def memset(self, eng, ap, val, w):
        self.op(eng, lambda e: e.memset(ap, val), (), w)

    def evac(self, out, in_, r, w):
        self.rr += 1
        self.cp("act" if self.rr % 2 else "dve", out, in_, r, w)


def _fft_tables(hf):
    n1 = np.arange(64)[:, None]
    k1 = np.arange(64)[None, :]
    th = 2 * np.pi * n1 * (k1 + 0.5) / 128
    D1 = np.concatenate([np.cos(th), -np.sin(th)], axis=1)
    n2 = np.arange(64)[:, None, None]
    kk1 = np.arange(64)[None, :, None]
    k2 = np.arange(64)[None, None, :]
    psi = 2 * np.pi * (n2 * (kk1 + 0.5) / NF + n2 * k2 / 64)
    Mre, Mim = np.cos(psi), -np.sin(psi)
    F2 = np.stack([Mre, Mim, -Mim], axis=2)
    kk2 = np.arange(64)[:, None]
    nn2 = np.arange(64)[None, :]
    th2 = 2 * np.pi * kk2 * nn2 / 64
    Ga = np.concatenate([np.cos(th2), np.sin(th2)], axis=1)
    Gb = np.concatenate([-np.sin(th2), np.cos(th2)], axis=1)
    I2 = np.stack([Ga, Gb], axis=1)
    k1c = np.arange(64)[:, None, None]
    n2c = np.arange(64)[None, :, None]
    n1i = np.arange(34)[None, None, :]
    n1v = 32 * hf - 1 + n1i
    n = 64 * n1v + n2c
    phi = 2 * np.pi * n * (k1c + 0.5) / NF
    valid = ((n1v >= 0) & (n1v < 64)).astype(np.float64)
    Tre = (2.0 / NF) * np.cos(phi) * valid
    Tim = -(2.0 / NF) * np.sin(phi) * valid
    I1 = np.stack([np.broadcast_to(Tre, (64, 64, 34)), np.broadcast_to(Tim, (64, 64, 34))], axis=2)
    f = lambda a: np.ascontiguousarray(a, dtype=np.float32)
    return f(D1), f(F2.reshape(64, -1)), f(I2.reshape(64, -1)), f(I1.reshape(64, -1))


def _rope_tables(tok):
    tok = np.asarray(tok)
    row = (tok // 64).astype(np.float64)
    col = (tok % 64).astype(np.float64)
    inv = 10000.0 ** (-np.arange(16, dtype=np.float64) / 16)
    cosT = np.zeros((128, tok.size))
    sinT = np.zeros((128, tok.size))
    for p in range(128):
        d = p % 64
        blk = d // 32
        i = d % 16
        second = (d % 32) >= 16
        pos = row if blk == 0 else col
        ang = pos * inv[i]
        cosT[p] = np.cos(ang)
        sinT[p] = np.sin(ang) if second else -np.sin(ang)
    return cosT.astype(np.float32), sinT.astype(np.float32)


def _perm_matrix():
    Pm = np.zeros((128, 128), np.float32)
    for m in range(128):
        d = m % 64
        k = m + 16 if (d % 32) < 16 else m - 16
        Pm[k, m] = 1.0
    return Pm


def _hyena_consts():
    t = np.linspace(0.0, 1.0, L, dtype=np.float32)
    bands = 16
    freqs = np.linspace(1e-4, bands - 1, bands, dtype=np.float32)
    ang = (np.float32(2.0 * math.pi / L) * np.arange(L, dtype=np.float32)[:, None]) * freqs[None, :]
    z = np.concatenate([t[:, None], np.cos(ang), -np.sin(ang)], axis=-1).astype(np.float32)
    max_decay = math.log(1e-2) / 0.3
    min_decay = math.log(1e-2) / 1.5
    deltas = np.abs(np.linspace(min_decay, max_decay, 512, dtype=np.float32))
    negd = (-deltas).reshape(4, 128).T
    return np.ascontiguousarray(z.T), t.reshape(1, L).copy(), np.ascontiguousarray(negd, dtype=np.float32)


def _pool_consts(hf):
    m0 = 2048 * hf
    tok = np.arange(m0 - 8, m0 + 2056)
    valid = ((tok >= 0) & (tok < L)).astype(np.float32)
    maskm = np.broadcast_to(valid[None, :], (128, NM)).copy()
    corr = np.ones((4, 2048), np.float32)
    for g, win in enumerate((2, 4, 8, 16)):
        t = np.arange(m0, m0 + 2048)
        lo = np.clip(t - win // 2, 0, L)
        hi = np.clip(t + win - win // 2, 0, L)
        corr[g] = 1.0 / (hi - lo)
    ce = np.concatenate([corr[:, :8], corr[:, -8:]], axis=1)
    ce = np.broadcast_to(ce[None], (128, 4, 16)).copy().astype(np.float32)
    tokx = np.arange(m0 - 9, m0 + 2057)
    vx = ((tokx >= 0) & (tokx < L)).astype(np.float32)
    maskx = np.concatenate([vx[:9], vx[-9:]])[None, :]
    maskx = np.broadcast_to(maskx, (128, 18)).copy()
    return maskm, ce, maskx


def build_program():
    nc = bass.Bass("TRN2", target_bir_lowering=False)
    es = ExitStack()
    ins = {}

    def I(name, shape, dt=F32):
        ins[name] = nc.dram_tensor(name, list(shape), dt, kind="ExternalInput").ap()
        return ins[name]

    x_all = I("x_all", (L, D)); x_m = I("x_m", (NMX, D)); ctx_in = I("ctx", (NCTX, D))
    cvec = I("cvec", (8, 128)); cctx = I("cctx", (8, 128))
    ada_w = I("ada_w", (2, D, 6 * D)); ada_b = I("ada_b", (2, 48, 128))
    norm1_g = I("norm1_g", (16, 128)); norm2_g = I("norm2_g", (16, 128))
    w_in = I("mix_w_in", (D, 3072)); b_in = I("mix_b_in", (24, 128)); b_in_row = I("b_in_row", (1, 3072))
    w_out = I("mix_w_out", (D, D)); b_out = I("mix_b_out", (8, 128))
    lamv = I("lamv", (1, 256)); subln = I("subln_g", (1, 128))
    cw = I("hy_conv_w", (36, 128)); cb = I("hy_conv_b", (12, 128))
    pw1 = I("hy_pos_w1", (33, 64)); pvec = I("hy_pvec", (4, 64)); pw2 = I("hy_pos_w2", (64, 64)); pw3 = I("hy_pos_w3", (64, 1024))
    hyb = I("hy_bias", (4, 128))
    pool_w = I("pool_w", (4, 256, 256)); pool_s = I("pool_scale", (8, 128))
    w1 = I("mlp_w1", (2, D, 4096)); w2 = I("mlp_w2", (2, 4096, D)); fin_g = I("final_g", (8, 128))
    t_ident = I("t_ident", (128, 128)); t_perm = I("t_perm", (128, 128))
    t_cos_a = I("t_cos_a", (128, L)); t_sin_a = I("t_sin_a", (128, L))
    t_cos_m = I("t_cos_m", (128, NM)); t_sin_m = I("t_sin_m", (128, NM))
    t_zT = I("t_zT", (33, L)); t_trow = I("t_trow", (1, L)); t_negd = I("t_negd", (128, 4))
    t_D1 = I("t_D1", (64, 128)); t_F2 = I("t_F2", (64, 64 * 3 * 64)); t_I2 = I("t_I2", (64, 256)); t_I1 = I("t_I1", (64, 64 * 2 * 34))
    t_maskm = I("t_maskm", (128, NM)); t_ce = I("t_ce", (128, 64)); t_maskx = I("t_maskx", (128, 18))

    out_d = nc.dram_tensor("out", [2048, D], F32, kind="ExternalOutput").ap()
    skind = "ExternalOutput" if DBG else "Internal"
    def S(name, shape, dt):
        return nc.dram_tensor(name, list(shape), dt, kind=skind).ap()
    KT_d = S("KT_d", (4, 128, NK), BF16)
    V_d = S("V_d", (34, 128, 512), BF16)
    QT_d = S("QT_d", (4, 128, NM), BF16)
    uT_d = S("uT_d", (512, L), BF16)
    x0T_d = S("x0T_d", (512, NM), BF16)
    ubT_d = S("ubT_d", (512, NM), BF16)
    filt_d = S("filt_d", (1024, L), BF16)
    F_d = S("F_d", (2, 128, 2 * 128 * 64), BF16)
    hyT_d = S("hyT_d", (512, NM), BF16)
    dbg_h = S("dbg_h", (5, 128, 8, NM), F32) if DBG else None
    attT_d = S("attT_d", (128, 4, NM), BF16)

    P = Prog(nc, es)
    sbuf = lambda st, name, shape, dt: st.enter_context(nc.sbuf_tensor(name, list(shape), dt))
    psum = lambda st, name, shape, dt: st.enter_context(nc.psum_tensor(name, list(shape), dt))

    VT = sbuf(es, "VT", (128, 320), F32)
    ident = sbuf(es, "ident", (128, 128), F32)
    identb = sbuf(es, "identb", (128, 128), BF16)
    onesb = sbuf(es, "onesb", (128, 128), BF16)
    mod = sbuf(es, "mod", (128, 2, 48, 2), F32)
    sc = sbuf(es, "sc", (128, 64), F32)
    pv = sbuf(es, "pv", (64, 8), F32)
    s2 = sbuf(es, "s2", (128, 8, 2), F32)
    PSA = psum(es, "psa", (128, 7, 512), F32)
    PS = [PSA[:, i, :] for i in range(7)]
    PSB = psum(es, "psb", (128, 1024), BF16)

    P.dma("sp", ident[:], t_ident, (), ["ident"])
    P.dma("pool", identb[:], t_ident, (), ["identb"])
    P.memset("dve", onesb[:], 1.0, ["onesb"])

    vt_cols = {}
    with ExitStack() as ph:
        stage = sbuf(ph, "stage", (128, 3, 128), F32)
        groups = [
            [("c", cvec, 8), ("cctx", cctx, 8), ("adab0", ada_b[0], 48), ("adab1", ada_b[1], 48)],
            [("n1g", norm1_g, 16), ("n2g", norm2_g, 16), ("bin", b_in, 24), ("bout", b_out, 8), ("subln", subln, 1), ("cw", cw, 36)],
            [("cb", cb, 12), ("hyb", hyb, 4), ("pools", pool_s, 8), ("fing", fin_g, 8)],
        ]
        col = 0
        for gi, g in enumerate(groups):
            r0 = 0
            for name, ap, nr in g:
                P.dma("sp", stage[r0:r0 + nr, gi, :], ap, (), ["stage%d" % gi])
                vt_cols[name] = col + r0
                r0 += nr
            P.tr(PS[0][:, 0:r0], stage[0:r0, gi, :], ident[0:r0, 0:r0], ["stage%d" % gi, "ident"], ["ps0"])
            P.cp("dve", VT[:, col:col + r0], PS[0][:, 0:r0], ["ps0"], ["VT"])
            col += r0
        P.dma("sp", stage[0:4, 0, 0:64], pvec, ["ps0"], ["stage0"])
        P.tr(PS[0][0:64, 0:4], stage[0:4, 0, 0:64], ident[0:4, 0:4], ["stage0", "ident"], ["ps0"])
        P.cp("dve", pv[:, 0:4], PS[0][0:64, 0:4], ["ps0"], ["pv"])
        P.tt("dve", pv[:, 4:5], pv[:, 0:1], pv[:, 1:2], ALU.mult, ["pv"], ["pv"])
        P.tt("dve", pv[:, 5:6], pv[:, 2:3], pv[:, 3:4], ALU.mult, ["pv"], ["pv"])

        V = lambda name, k=0: VT[:, vt_cols[name] + k: vt_cols[name] + k + 1]
        P.act(s2[:, :, 0], VT[:, vt_cols["c"]:vt_cols["c"] + 8], AF.Silu, ["VT"], ["s2"])
        P.act(s2[:, :, 1], VT[:, vt_cols["cctx"]:vt_cols["cctx"] + 8], AF.Silu, ["VT", "s2"], ["s2"])
        adab = [sbuf(ph, "adab%d" % i, (128, 8, 512), F32) for i in range(2)]
        modrow = sbuf(ph, "modrow", (2, 6144), F32)
        ada_cnt = [0]

        def ada_group(li, g, adab, modrow):
            bsel = ada_cnt[0] % 2
            ada_cnt[0] += 1
            P.dma("sp", adab[bsel][:], ada_w[li].rearrange("(k p) c -> p k c", p=128)[:, :, g * 512:(g + 1) * 512], (), ["adab%d" % bsel])
            pr = PS[3 + g % 2]
            for k in range(8):
                P.mm(pr[0:2, :], s2[:, k, :], adab[bsel][:, k, :], k == 0, k == 7, ["adab%d" % bsel, "s2"], ["pr%d" % (g % 2)])
            P.evac(modrow[:, g * 512:(g + 1) * 512], pr[0:2, :], ["pr%d" % (g % 2)], ["modrow"])

        def ada_finish(li, modrow):
            pm = PS[2]
            for q in range(48):
                P.tr(pm[:, 2 * q:2 * q + 2], modrow[0:2, q * 128:(q + 1) * 128], ident[0:2, 0:2], ["modrow", "ident"], ["pm"], signal=(q == 47))
            ab = VT[:, vt_cols["adab%d" % li]:vt_cols["adab%d" % li] + 48]
            pmv = pm[:, 0:96].rearrange("p (q t) -> p q t", t=2)
            P.tt("dve", mod[:, li, :, 0], pmv[:, :, 0], ab, ALU.add, ["pm", "VT"], ["mod"])
            P.tt("dve", mod[:, li, :, 1], pmv[:, :, 1], ab, ALU.add, ["pm", "VT", "mod"], ["mod"])

        for g in range(12):
            ada_group(0, g, adab, modrow)
        ada_finish(0, modrow)
        SC = {}
        def scal(name, n):
            SC[name] = len(SC_alloc)
            SC_alloc.extend([0] * n)
            return sc[:, SC[name]:SC[name] + n]
        SC_alloc = []
        def g1(li): return VT[:, vt_cols["n1g"] + 8 * li: vt_cols["n1g"] + 8 * li + 8]
        def g2(li): return VT[:, vt_cols["n2g"] + 8 * li: vt_cols["n2g"] + 8 * li + 8]
        def M(li, m, who=0): return mod[:, li, 8 * m:8 * m + 8, who]
        for li in range(2):
            a = scal("A1_%d" % li, 8)
            if li == 0:
                P.stt(a, M(li, 1), 1.0, g1(li), ALU.add, ALU.mult, ["mod", "VT"], ["sc"])
            a = scal("A2_%d" % li, 8)
            if li == 0:
                P.stt(a, M(li, 4), 1.0, g2(li), ALU.add, ALU.mult, ["mod", "VT", "sc"], ["sc"])
        a = scal("A1c", 8)
        P.stt(a, M(0, 1, 1), 1.0, g1(0), ALU.add, ALU.mult, ["mod", "VT", "sc"], ["sc"])
        a = scal("GB", 8)
        P.tt("dve", a, M(0, 2), VT[:, vt_cols["bout"]:vt_cols["bout"] + 8], ALU.mult, ["mod", "VT", "sc"], ["sc"])
        a = scal("GS", 8)
        a = scal("g08", 1)
        P.ts("dve", a, V("subln"), 0.8, None, ALU.mult, None, ["VT", "sc"], ["sc"])
        lt = sbuf(ph, "lt", (128, 4, 64), F32)
        lj = sbuf(ph, "lj", (128, 64), F32)
        P.dma("sp", lt[:].rearrange("p a b -> p (a b)"), lamv.partition_broadcast(128), (), ["lt"])
        e12 = scal("e12", 2)
        for i in range(2):
            P.tt("dve", lj[:], lt[:, 2 * i, :], lt[:, 2 * i + 1, :], ALU.mult, ["lt"], ["lj"])
            P.act(lj[:], lj[:], AF.Copy, ["lj"], ["lj", "sc"], accum=e12[:, i:i + 1])
        P.act(e12, e12, AF.Exp, ["sc"], ["sc"])
        nl = scal("neglam", 1)
        P.tt("dve", nl, e12[:, 1:2], e12[:, 0:1], ALU.subtract, ["sc"], ["sc"])
        P.ts("dve", nl, nl, -0.2, None, ALU.add, None, ["sc"], ["sc"])
    P.barrier()
    g1_, g2_, M_ = g1, g2, M
    SCv = lambda name, k=0, n=1: sc[:, SC[name] + k: SC[name] + k + n]
    MODv = lambda li, m, k, who=0: mod[:, li, 8 * m + k: 8 * m + k + 1, who]

    MT = [(0, 512), (512, 512), (1024, 512), (1536, 512), (2048, 16)]
    MTX = [(0, 512), (512, 512), (1024, 512), (1536, 512), (2048, 18)]

    def norm_tok(ph, rows_ap, ntok, Acol, Bfn, outT, okey, tag, hook=None):
        xt = [sbuf(ph, "xt%s%d" % (tag, i), (128, D), F32) for i in range(2)]
        xn = [sbuf(ph, "xn%s%d" % (tag, i), (128, D), F32) for i in range(2)]
        st = sbuf(ph, "nst" + tag, (128, 8), F32)
        nch = (ntok + 127) // 128

        def front(r):
            r0 = r * 128
            nr = min(128, ntok - r0)
            b = r % 2
            P.dma("sp", xt[b][0:nr, :], rows_ap[r0:r0 + nr, :], (), ["xt%s%d" % (tag, b)])
            P.act(xn[b][0:nr, :], xt[b][0:nr, :], AF.Square, ["xt%s%d" % (tag, b)], ["xn%s%d" % (tag, b), "nst" + tag], accum=st[0:nr, 0:1])
            P.act(st[0:nr, 1:2], st[0:nr, 0:1], AF.Sqrt, ["nst" + tag], ["nst" + tag], bias=1e-6, scale=1.0 / D)
            P.recip(st[0:nr, 2:3], st[0:nr, 1:2], ["nst" + tag], ["nst" + tag])
            P.ts("dve", xn[b][0:nr, :], xt[b][0:nr, :], st[0:nr, 2:3], None, ALU.mult, None, ["xt%s%d" % (tag, b), "nst" + tag], ["xn%s%d" % (tag, b)])

        def back(r):
            r0 = r * 128
            nr = min(128, ntok - r0)
            b = r % 2
            for half in range(2):
                pt = PS[half]
                for kk in range(4):
                    k = half * 4 + kk
                    P.tr(pt[:, kk * 128:kk * 128 + nr], xn[b][0:nr, k * 128:(k + 1) * 128], ident[0:nr, 0:nr],
                         ["xn%s%d" % (tag, b), "ident"], ["ps%d" % half], signal=(kk == 3))
                for kk in range(4):
                    k = half * 4 + kk
                    if kk % 2 == 0:
                        P.ts("dve", outT[:, k, r0:r0 + nr], pt[:, kk * 128:kk * 128 + nr], Acol(k), Bfn(k), ALU.mult, ALU.add, ["ps%d" % half, "sc", "mod"], [okey])
                    else:
                        P.act(outT[:, k, r0:r0 + nr], pt[:, kk * 128:kk * 128 + nr], AF.Identity, ["ps%d" % half, "sc", "mod"], [okey], bias=Bfn(k), scale=Acol(k))

        front(0)
        for r in range(nch):
            if r + 1 < nch:
                front(r + 1)
            back(r)
            if hook is not None:
                hook(r)

    with ExitStack() as ph:
        aT_all = sbuf(ph, "aT_all", (128, 8, L), BF16)
        aT_m = sbuf(ph, "aT_m", (128, 8, NMX), BF16)
        aT_c = sbuf(ph, "aT_c", (128, 8, NCTX), BF16)
        with ExitStack() as ph0:
            adab1 = [sbuf(ph0, "adab1_%d" % i, (128, 8, 512), F32) for i in range(2)]
            modrow1 = sbuf(ph0, "modrow1", (2, 6144), F32)

            def hook(r):
                if r % 2 == 1 and r // 2 < 12:
                    ada_group(1, r // 2, adab1, modrow1)
            norm_tok(ph0, x_all, L, lambda k: SCv("A1_0", k), lambda k: MODv(0, 0, k), aT_all, "aT_all", "a", hook=hook)
            ada_finish(1, modrow1)
            P.stt(SCv("A1_1", 0, 8), M_(1, 1), 1.0, g1_(1), ALU.add, ALU.mult, ["mod", "VT", "sc"], ["sc"])
            P.stt(SCv("A2_1", 0, 8), M_(1, 4), 1.0, g2_(1), ALU.add, ALU.mult, ["mod", "VT", "sc"], ["sc"])
            P.tt("dve", SCv("GS", 0, 8), M_(1, 2), VT[:, vt_cols["pools"]:vt_cols["pools"] + 8], ALU.mult, ["mod", "VT", "sc"], ["sc"])
            norm_tok(ph0, x_m, NMX, lambda k: SCv("A1_0", k), lambda k: MODv(0, 0, k), aT_m, "aT_m", "m")
            norm_tok(ph0, ctx_in, NCTX, lambda k: SCv("A1c", k), lambda k: MODv(0, 0, k, 1), aT_c, "aT_c", "c")
        P.barrier()
        wb = [sbuf(ph, "wb%d" % i, (128, 8, 128), BF16) for i in range(4)]
        wseq = []
        for h in range(4):
            wseq += [h, 4 + h]
        for jj in range(4):
            wseq += [12 + jj, 16 + jj, 20 + jj, 16 + jj, 20 + jj, 16 + jj, 20 + jj]
        wbc = [0]
        wiss = [0]
        def _issue_w():
            if wiss[0] < len(wseq):
                i = wiss[0] % 4
                j = wseq[wiss[0]]
                wiss[0] += 1
                P.dma("pool", wb[i][:], w_in.rearrange("(k p) c -> p k c", p=128)[:, :, j * 128:(j + 1) * 128], (), ["wb%d" % i])
        def load_w(j):
            while wiss[0] < min(wbc[0] + 3, len(wseq)):
                _issue_w()
            i = wbc[0] % 4
            assert wseq[wbc[0]] == j, (wseq[wbc[0]], j)
            wbc[0] += 1
            return wb[i], "wb%d" % i
        permb = sbuf(ph, "permb", (128, 128), BF16)
        P.dma("pool", permb[:], t_perm, (), ["permb"])
        ropet = [sbuf(ph, "ropet%d" % i, (128, 2, 512), F32) for i in range(2)]
        qraw = [sbuf(ph, "qraw%d" % i, (128, 512), BF16) for i in range(2)]
        rt1 = [sbuf(ph, "rt1%d" % i, (128, 512), F32) for i in range(2)]
        rt2 = [sbuf(ph, "rt2%d" % i, (128, 512), F32) for i in range(2)]
        kst = sbuf(ph, "kst", (128, NK), BF16)
        tcnt = [0]

        def proj_rope(wt, wkey, bias_col, src, skey, c_off, tiles, cos_d, sin_d, dst, dkey, d_off):
            for (c0, w) in tiles:
                i = tcnt[0] % 2
                tcnt[0] += 1
                pq = PS[2 + i]
                P.dma("sp", ropet[i][:, 0, 0:w], cos_d[:, c0:c0 + w], (), ["ropet%d" % i])
                P.dma("sp", ropet[i][:, 1, 0:w], sin_d[:, c0:c0 + w], (), ["ropet%d" % i])
                for k in range(8):
                    P.mm(pq[:, 0:w], wt[:, k, :], src[:, k, c_off + c0:c_off + c0 + w], k == 0, k == 7, [wkey, skey], ["pq%d" % i])
                P.act(qraw[i][:, 0:w], pq[:, 0:w], AF.Identity, ["pq%d" % i, "VT"], ["qraw%d" % i], bias=bias_col)
                psw = PS[4 + i]
                P.mm(psw[:, 0:w], permb[:], qraw[i][:, 0:w], True, True, ["permb", "qraw%d" % i], ["psw%d" % i])
                P.tt("pool", rt1[i][:, 0:w], qraw[i][:, 0:w], ropet[i][:, 0, 0:w], ALU.mult, ["qraw%d" % i, "ropet%d" % i], ["rt1%d" % i])
                P.tt("dve", rt2[i][:, 0:w], psw[:, 0:w], ropet[i][:, 1, 0:w], ALU.mult, ["psw%d" % i, "ropet%d" % i], ["rt2%d" % i])
                P.tt("dve", dst[:, d_off + c0:d_off + c0 + w], rt1[i][:, 0:w], rt2[i][:, 0:w], ALU.add, ["rt1%d" % i, "rt2%d" % i], [dkey])

        for h in range(4):
            wt, wkey = load_w(h)
            proj_rope(wt, wkey, V("bin", h), aT_m, "aT_m", 1, MT, t_cos_m, t_sin_m, kst, "kst", 0)
            P.dma("sp", QT_d[h], kst[:, 0:NM], ["kst"], ["QT_d"])
            wt, wkey = load_w(4 + h)
            pq = PS[6]
            for k in range(8):
                P.mm(pq[:, 0:NCTX], wt[:, k, :], aT_c[:, k, :], k == 0, k == 7, [wkey, "aT_c"], ["pq6"])
            P.act(kst[:, 0:NCTX], pq[:, 0:NCTX], AF.Identity, ["pq6", "VT"], ["kst"], bias=V("bin", 4 + h))
            proj_rope(wt, wkey, V("bin", 4 + h), aT_all, "aT_all", 0, [(i * 512, 512) for i in range(8)], t_cos_a, t_sin_a, kst, "kst", NCTX)
            P.dma("sp", KT_d[h], kst[:], ["kst"], ["KT_d"])
        wv = sbuf(ph, "wv", (128, 8, 512), BF16)
        bv = sbuf(ph, "bv", (128, 512), F32)
        P.dma("pool", wv[:], w_in.rearrange("(k p) c -> p k c", p=128)[:, :, 1024:1536], (), ["wv"])
        P.dma("sp", bv[:], b_in_row[0:1, 1024:1536].partition_broadcast(128), (), ["bv"])
        vst = [sbuf(ph, "vst%d" % i, (128, 512), BF16) for i in range(2)]
        for kc in range(34):
            i = kc % 2
            pv_ = PS[2 + i]
            for k in range(8):
                lhs = aT_c[:, k, kc * 128:(kc + 1) * 128] if kc < 2 else aT_all[:, k, (kc - 2) * 128:(kc - 1) * 128]
                P.mm(pv_[:], lhs, wv[:, k, :], k == 0, k == 7, ["wv", "aT_c", "aT_all"], ["pq%d" % i])
            P.tt("dve", vst[i][:], pv_[:], bv[:], ALU.add, ["pq%d" % i, "bv"], ["vst%d" % i])
            P.dma("sp", V_d[kc], vst[i][:], ["vst%d" % i], ["V_d"])
        hp = [sbuf(ph, "hp%d" % i, (128, NMX), F32) for i in range(3)]
        ha = sbuf(ph, "ha", (128, NMX), F32)
        hb_ = sbuf(ph, "hb_", (128, NMX), F32)
        hx = sbuf(ph, "hx", (128, NMX), BF16)
        mx = sbuf(ph, "mx", (128, 18), F32)
        P.dma("sp", mx[:], t_maskx, (), ["mx"])

        def proj_plain(j, src, skey, pieces, dst, dkey):
            wt, wkey = load_w(j)
            for (s0, w, d0) in pieces:
                i = tcnt[0] % 2
                tcnt[0] += 1
                pq = PS[2 + i]
                for k in range(8):
                    P.mm(pq[:, 0:w], wt[:, k, :], src[:, k, s0:s0 + w], k == 0, k == 7, [wkey, skey], ["pq%d" % i])
                P.act(dst[:, d0:d0 + w], pq[:, 0:w], AF.Identity, ["pq%d" % i, "VT"], [dkey], bias=V("bin", j))

        def conv3(src, skey, n, jcol, dst, dkey, tmp, tkey):
            w0, w1_, w2_ = V("cw", jcol), V("cw", 12 + jcol), V("cw", 24 + jcol)
            P.ts("dve", tmp[:, 0:n], src[:, 1:n + 1], w1_, V("cb", jcol), ALU.mult, ALU.add, [skey, "VT"], [tkey])
            P.stt(tmp[:, 0:n], src[:, 0:n], w0, tmp[:, 0:n], ALU.mult, ALU.add, [skey, tkey, "VT"], [tkey])
            P.stt(dst[:, 0:n], src[:, 2:n + 2], w2_, tmp[:, 0:n], ALU.mult, ALU.add, [skey, tkey, "VT"], [dkey])

        mine_pieces = [(c0, w, c0) for (c0, w) in MTX]
        for jj in range(4):
            for which in range(3):
                j = 12 + 4 * which + jj
                proj_plain(j, aT_m, "aT_m", mine_pieces, hp[which], "hp%d" % which)
                P.tt("pool", hp[which][:, 0:9], hp[which][:, 0:9], mx[:, 0:9], ALU.mult, ["hp%d" % which, "mx"], ["hp%d" % which])
                P.tt("pool", hp[which][:, NMX - 9:NMX], hp[which][:, NMX - 9:NMX], mx[:, 9:18], ALU.mult, ["hp%d" % which, "mx"], ["hp%d" % which])
            conv3(hp[0], "hp0", NM, jj, hx, "hx", ha, "ha")
            P.dma("sp", x0T_d[jj * 128:(jj + 1) * 128, :], hx[:, 0:NM], ["hx"], ["x0T_d"])
            conv3(hp[1], "hp1", NM, 4 + jj, hb_, "hb_", ha, "ha")
            conv3(hp[2], "hp2", NM, 8 + jj, hp[0], "hp0", ha, "ha")
            P.stt(hx[:, 0:NM], hb_[:, 0:NM], V("hyb", jj), hp[0][:, 0:NM], ALU.mult, ALU.mult, ["hb_", "hp0", "VT"], ["hx"])
            P.dma("sp", ubT_d[jj * 128:(jj + 1) * 128, :], hx[:, 0:NM], ["hx"], ["ubT_d"])
            for hh in range(2):
                if hh == 0:
                    pieces = [(0, 512, 1), (512, 512, 513), (1024, 512, 1025), (1536, 512, 1537), (2048, 1, 2049)]
                    zcol = 0
                else:
                    pieces = [(2047, 1, 0), (2048, 512, 1), (2560, 512, 513), (3072, 512, 1025), (3584, 512, 1537)]
                    zcol = 2049
                for which in (1, 2):
                    j = 12 + 4 * which + jj
                    proj_plain(j, aT_all, "aT_all", pieces, hp[which], "hp%d" % which)
                    P.memset("pool", hp[which][:, zcol:zcol + 1], 0.0, ["hp%d" % which])
                conv3(hp[1], "hp1", 2048, 4 + jj, hb_, "hb_", ha, "ha")
                conv3(hp[2], "hp2", 2048, 8 + jj, hp[0], "hp0", ha, "ha")
                P.tt("dve", hx[:, 0:2048], hb_[:, 0:2048], hp[0][:, 0:2048], ALU.mult, ["hb_", "hp0"], ["hx"])
                P.dma("sp", uT_d[jj * 128:(jj + 1) * 128, hh * 2048:(hh + 1) * 2048], hx[:, 0:2048], ["hx"], ["uT_d"])
    P.barrier()

    if KSTOP == '1':
        P.barrier(); P.emit(); es.close(); return nc
    with ExitStack() as ph:
        attT = sbuf(ph, "attT", (128, 4, NM), BF16)
        Vs = sbuf(ph, "Vs", (128, 34, 512), BF16)
        P.dma("sp", Vs[:], V_d.rearrange("k p c -> p k c"), ["V_d"], ["Vs"])
        KTs = [sbuf(ph, "KTs%d" % i, (128, NK), BF16) for i in range(2)]
        QTs = [sbuf(ph, "QTs%d" % i, (128, NM), BF16) for i in range(2)]
        rz = sbuf(ph, "rz", (128, 2, 512), F32)
        to = sbuf(ph, "to", (128, 2, 512), F32)
        osb = sbuf(ph, "osb", (128, 512), F32)
        sqb = sbuf(ph, "sqb", (128, 512), BF16)
        rsb = sbuf(ph, "rsb", (128, 512), F32)
        ZB = [PS[6], PSB[:].bitcast(F32)]
        zD = sbuf(ph, "zD", (128, 2, 512), F32)
        zb2 = sbuf(ph, "zb2", (128, 2, 512), BF16)
        pT2 = [sbuf(ph, "pTw%d" % i, (128, 2, 512), BF16) for i in range(3)]
        steps = []
        for h in range(4):
            for (q0, w) in MT:
                for kc in range(34):
                    steps.append((h, q0, w, kc))

        def issue_S(idx):
            h, q0, w, kc = steps[idx]
            hb = h % 2
            if q0 == 0 and kc == 0:
                P.dma("sp", KTs[hb][:], KT_d[h], ["KT_d"], ["KTs%d" % hb])
                P.dma("sp", QTs[hb][:], QT_d[h], ["QT_d"], ["QTs%d" % hb])
            p_ = idx % 2
            for c in range(2):
                P.mm(PSA[:, 2 * p_ + c, 0:w], KTs[hb][c * 64:(c + 1) * 64, kc * 128:(kc + 1) * 128], QTs[hb][c * 64:(c + 1) * 64, q0:q0 + w],
                     True, True, ["KTs%d" % hb, "QTs%d" % hb], ["S%d" % p_], signal=(c == 1))

        issue_S(0)
        for idx, (h, q0, w, kc) in enumerate(steps):
            p_ = idx % 2
            pb = idx % 3
            P.act(pT2[pb][:, :, 0:w], PSA[:, 2 * p_:2 * p_ + 2, 0:w], AF.Exp, ["S%d" % p_], ["pT%d" % pb], scale=0.125)
            if idx + 1 < len(steps):
                issue_S(idx + 1)
            for c in range(2):
                P.mm(PS[4 + c][:, 0:w], Vs[:, kc, h * 128:(h + 1) * 128], pT2[pb][:, c, 0:w], kc == 0, kc == 33, ["Vs", "pT%d" % pb], ["O%d" % c], signal=(c == 1))
            if kc % 3 == 0:
                if kc == 0:
                    P.cp("dve", zD[:, :, 0:w], pT2[pb][:, :, 0:w], ["pT%d" % pb], ["zD"])
                else:
                    P.tt("dve", zD[:, :, 0:w], zD[:, :, 0:w], pT2[pb][:, :, 0:w], ALU.add, ["pT%d" % pb, "zD"], ["zD"])
            else:
                for c in range(2):
                    P.mm(ZB[c][:, 0:w], onesb[:], pT2[pb][:, c, 0:w], kc == 1, False, ["onesb", "pT%d" % pb], ["Z%d" % c], signal=(c == 1))
            if kc == 33:
                P.cp("dve", zb2[:, :, 0:w], zD[:, :, 0:w], ["zD"], ["zb"])
                for c in range(2):
                    P.mm(ZB[c][:, 0:w], onesb[:], zb2[:, c, 0:w], False, True, ["onesb", "zb"], ["Z%d" % c], signal=(c == 1))
                for c in range(2):
                    P.recip(rz[:, c, 0:w], ZB[c][:, 0:w], ["Z%d" % c], ["rz%d" % c])
                    P.tt("dve", to[:, c, 0:w], PS[4 + c][:, 0:w], rz[:, c, 0:w], ALU.mult, ["O%d" % c, "rz%d" % c], ["to%d" % c])
                P.stt(osb[:, 0:w], to[:, 1, 0:w], SCv("neglam"), to[:, 0, 0:w], ALU.mult, ALU.add, ["to0", "to1", "sc"], ["osb"])
                P.act(sqb[:, 0:w], osb[:, 0:w], AF.Square, ["osb"], ["sqb"])
                P.mm(PS[6][:, 0:w], onesb[:], sqb[:, 0:w], True, True, ["onesb", "sqb"], ["Z0"])
                P.act(rsb[:, 0:w], PS[6][:, 0:w], AF.Sqrt, ["Z0"], ["rsb"], bias=1e-5, scale=1.0 / 128)
                P.recip(rsb[:, 0:w], rsb[:, 0:w], ["rsb"], ["rsb"])
                P.stt(attT[:, h, q0:q0 + w], osb[:, 0:w], SCv("g08"), rsb[:, 0:w], ALU.mult, ALU.mult, ["osb", "rsb", "sc"], ["attT"])
        P.dma("sp", attT_d, attT[:], ["attT"], ["attT_d"])
    P.barrier()

    if KSTOP == '2':
        P.barrier(); P.emit(); es.close(); return nc
    with ExitStack() as ph:
        zT = sbuf(ph, "zT", (33, L), F32)
        P.dma("sp", zT[:], t_zT, (), ["zT"])
        trow = sbuf(ph, "trow", (128, L), F32)
        P.dma("sp", trow[:], t_trow.partition_broadcast(128), (), ["trow"])
        negd = sbuf(ph, "negd", (128, 4), F32)
        P.dma("sp", negd[:], t_negd, (), ["negd"])
        w1s = sbuf(ph, "w1s", (33, 64), F32)
        w2s = sbuf(ph, "w2s", (64, 64), F32)
        w3s = sbuf(ph, "w3s", (64, 1024), F32)
        P.dma("sp", w1s[:], pw1, (), ["w1s"])
        P.dma("sp", w2s[:], pw2, (), ["w2s"])
        P.dma("sp", w3s[:], pw3, (), ["w3s"])
        hd = [sbuf(ph, "hd%d" % i, (64, L), F32) for i in range(2)]
        r1 = sbuf(ph, "r1", (64, L), F32)
        r2 = sbuf(ph, "r2", (64, L), F32)
        PI = float(np.pi)

        def sin_layer(wt, wkey, kdim, src, skey, dst, dkey, fcol, fbcol):
            for t in range(8):
                pq = PS[t % 2]
                P.mm(pq[0:64, :], wt[0:kdim, :], src[0:kdim, t * 512:(t + 1) * 512], True, True, [wkey, skey], ["fp%d" % (t % 2)])
                P.act(dst[:, t * 512:(t + 1) * 512], pq[0:64, :], AF.Identity, ["fp%d" % (t % 2), "pv"], [dkey], bias=pv[:, fbcol:fbcol + 1], scale=pv[:, fcol:fcol + 1])
            P.ts("dve", r1[:], dst[:], PI, -2 * PI, ALU.is_gt, ALU.mult, [dkey], ["r1"])
            P.ts("dve", r2[:], dst[:], -PI, 2 * PI, ALU.is_lt, ALU.mult, [dkey], ["r2"])
            P.tt("dve", r1[:], r1[:], r2[:], ALU.add, ["r1", "r2"], ["r1"])
            P.tt("dve", dst[:], dst[:], r1[:], ALU.add, [dkey, "r1"], [dkey])
            P.ts("dve", dst[:], dst[:], 3.1415925, -3.1415925, ALU.min, ALU.max, [dkey], [dkey])
            P.act(dst[:], dst[:], AF.Sin, [dkey], [dkey])

        sin_layer(w1s, "w1s", 33, zT, "zT", hd[0], "hd0", 1, 4)
        sin_layer(w2s, "w2s", 64, hd[0], "hd0", hd[1], "hd1", 3, 5)
        dec = sbuf(ph, "dec", (128, L), F32)
        fst = [sbuf(ph, "fst%d" % i, (128, L), BF16) for i in range(2)]
        for cc in range(4):
            P.act(dec[:], trow[:], AF.Exp, ["trow", "negd"], ["dec"], scale=negd[:, cc:cc + 1])
            for dr in range(2):
                jcol = dr * 4 + cc
                fb = jcol % 2
                for t in range(8):
                    pq = PS[2 + t % 2]
                    P.mm(pq[:], w3s[:, jcol * 128:(jcol + 1) * 128], hd[1][:, t * 512:(t + 1) * 512], True, True, ["w3s", "hd1"], ["fq%d" % (t % 2)])
                    P.tt("dve", fst[fb][:, t * 512:(t + 1) * 512], pq[:], dec[:, t * 512:(t + 1) * 512], ALU.mult, ["fq%d" % (t % 2), "dec"], ["fst%d" % fb])
                if dr == 1:
                    P.memset("dve", fst[fb][:, 0:1], 0.0, ["fst%d" % fb])
                P.dma("sp", filt_d[dr * 512 + cc * 128: dr * 512 + (cc + 1) * 128, :], fst[fb][:], ["fst%d" % fb], ["filt_d"])
    P.barrier()

    if KSTOP == '3a':
        P.barrier(); P.emit(); es.close(); return nc
    with ExitStack() as ph:
        D1b = sbuf(ph, "D1b", (128, 128), BF16)
        F2b = sbuf(ph, "F2b", (128, 64, 3, 64), BF16)
        I2b = sbuf(ph, "I2b", (128, 2, 128), BF16)
        I1b = sbuf(ph, "I1b", (128, 64, 2, 34), BF16)
        for h in range(2):
            hs = slice(64 * h, 64 * h + 64)
            P.dma("pool", D1b[hs, :], t_D1, (), ["D1b"])
            P.dma("pool", F2b[hs].rearrange("p a b c -> p (a b c)"), t_F2, (), ["F2b"])
            P.dma("pool", I2b[hs].rearrange("p a b -> p (a b)"), t_I2, (), ["I2b"])
            P.dma("pool", I1b[hs].rearrange("p a b c -> p (a b c)"), t_I1, (), ["I1b"])
        xsb = [sbuf(ph, "xs%d" % i, (128, 128, 64), BF16) for i in range(2)]
        Ab = sbuf(ph, "Ab", (128, 2, 64, 128), BF16)
        Xb = sbuf(ph, "Xb", (128, 2, 64, 64), BF16)
        Fb = sbuf(ph, "Fb", (128, 2, 128, 64), BF16)
        Yb = sbuf(ph, "Yb", (128, 2, 64, 64), BF16)
        t1 = sbuf(ph, "t1", (128, 2048), F32)
        t2 = sbuf(ph, "t2", (128, 2048), F32)
        yT = sbuf(ph, "yT", (128, 34, 64), F32)
        x0ts = [sbuf(ph, "x0t%d" % i, (128, NM), BF16) for i in range(2)]
        ubts = [sbuf(ph, "ubt%d" % i, (128, NM), BF16) for i in range(2)]
        scnt = [0]
        HS = [slice(0, 64), slice(64, 128)]
        TP = [None, (64, 64)]

        lcnt = [0]

        def load_xs(src_fn, nch):
            si = lcnt[0] % 2
            lcnt[0] += 1
            xs = xsb[si]
            for h in range(2):
                for q4 in range(nch // 32):
                    P.dma("sp", xs[HS[h], q4 * 32:(q4 + 1) * 32, :], src_fn(h)[q4 * 32:(q4 + 1) * 32, :].rearrange("c (a b) -> a c b", b=64),
                          ["uT_d", "filt_d"], ["xs%d" % si])

        def forward(src_fn, nch, mode):
            si = scnt[0] % 2
            scnt[0] += 1
            xs = xsb[si]
            xkey = "xs%d" % si
            for c0 in range(0, nch, 4):
                pa = PS[(c0 // 4) % 2]
                for i in range(4):
                    for h in range(2):
                        P.mm(pa[HS[h], i * 128:(i + 1) * 128], xs[HS[h], c0 + i, :], D1b[HS[h], :], True, True, [xkey, "D1b"], ["pa%d" % ((c0 // 4) % 2)],
                             signal=(i == 3 and h == 1), tp=TP[h])
                P.evac(Ab[:, :, :, c0:c0 + 4].rearrange("p r k c -> p (r k) c"), pa.rearrange("p (i m) -> p m i", i=4),
                       ["pa%d" % ((c0 // 4) % 2)], ["Ab"])
            KG = 512 // (2 * nch)
            for kg in range(64 // KG):
                px = PS[2 + kg % 2]
                pxv = px.rearrange("p (k r c) -> p k r c", k=KG, r=2)
                for i in range(KG):
                    k1 = KG * kg + i
                    last = (i == KG - 1)
                    for (ro, tb, ai, st_, sp_) in ((0, 0, 0, True, False), (0, 2, 1, False, True), (1, 1, 0, True, False), (1, 0, 1, False, True)):
                        for h in range(2):
                            P.mm(pxv[HS[h], i, ro, :], F2b[HS[h], k1, tb, :], Ab[HS[h], ai, k1, 0:nch], st_, sp_, ["F2b", "Ab"], ["px%d" % (kg % 2)],
                                 signal=(last and ro == 1 and ai == 1 and h == 1), tp=TP[h])
                pin = px.rearrange("p (k r c) -> p r c k", k=KG, r=2)
                ks = slice(KG * kg, KG * kg + KG)
                if mode == "hf":
                    P.evac(Fb[:, :, 0:nch, ks], pin, ["px%d" % (kg % 2)], ["Fb"])
                elif mode == "u":
                    P.evac(Xb[:, :, 0:nch, ks], pin, ["px%d" % (kg % 2)], ["Xb"])
                else:
                    P.tt("dve", Fb[:, 0, 0:nch, ks], pin[:, 0], Fb[:, 0, 0:nch, ks], ALU.add, ["px%d" % (kg % 2), "Fb"], ["Fb"])
                    P.stt(Fb[:, 1, 0:nch, ks], pin[:, 1], -1.0, Fb[:, 1, 0:nch, ks], ALU.mult, ALU.add, ["px%d" % (kg % 2), "Fb"], ["Fb"])

        fsrc = []
        for dp in range(2):
            fsrc.append((lambda h, dp=dp: filt_d[dp * 256 + h * 128: dp * 256 + (h + 1) * 128, :], 128, "hf"))
            fsrc.append((lambda h, dp=dp: filt_d[512 + dp * 256 + h * 128: 512 + dp * 256 + (h + 1) * 128, :], 128, "hb"))
        usrc = [(lambda h, jj=jj: uT_d[jj * 128 + h * 64: jj * 128 + (h + 1) * 64, :], 64, "u") for jj in range(4)]
        allsrc = fsrc + usrc
        load_xs(allsrc[0][0], allsrc[0][1])
        for pi_ in range(4):
            load_xs(allsrc[pi_ + 1][0], allsrc[pi_ + 1][1])
            forward(*allsrc[pi_])
            if pi_ % 2 == 1:
                P.dma("sp", F_d[pi_ // 2], Fb[:].rearrange("p r c k -> p (r c k)"), ["Fb"], ["F_d"])

        def load_u(jj):
            fo = 64 * (jj % 2)
            Fsrc = F_d[jj // 2].rearrange("p (r c k) -> p r c k", r=2, c=128)
            for h in range(2):
                P.dma("sp", Fb[HS[h], :, fo:fo + 64, :], Fsrc[64 * (jj % 2):64 * (jj % 2) + 64, :, 64 * h:64 * h + 64, :], ["F_d", "Fb"], ["Fv%d" % (jj % 2)])
            P.dma("sp", x0ts[jj % 2][:], x0T_d[jj * 128:(jj + 1) * 128, :], ["x0T_d"], ["x0t%d" % (jj % 2)])
            P.dma("sp", ubts[jj % 2][:], ubT_d[jj * 128:(jj + 1) * 128, :], ["ubT_d"], ["ubt%d" % (jj % 2)])

        load_u(0)
        for jj in range(4):
            fo = 64 * (jj % 2)
            Fv = Fb[:, :, fo:fo + 64, :]
            fkey = "Fv%d" % (jj % 2)
            x0t, ubt = x0ts[jj % 2], ubts[jj % 2]
            xkey_, ukey_ = "x0t%d" % (jj % 2), "ubt%d" % (jj % 2)
            if jj + 1 < 4:
                load_xs(allsrc[4 + jj + 1][0], 64)
                load_u(jj + 1)
            forward(*allsrc[4 + jj])
            Xf = Xb[:].rearrange("p r c k -> p r (c k)")
            Ff = Fv.rearrange("p r c k -> p r (c k)")
            Yf = Yb[:].rearrange("p r c k -> p r (c k)")
            for hv in range(2):
                sl = slice(hv * 2048, (hv + 1) * 2048)
                P.tt("dve", t1[:], Xf[:, 0, sl], Ff[:, 0, sl], ALU.mult, ["Xb", fkey], ["t1"])
                P.tt("pool", t2[:], Xf[:, 1, sl], Ff[:, 1, sl], ALU.mult, ["Xb", fkey], ["t2"])
                P.tt("dve", Yf[:, 0, sl], t1[:], t2[:], ALU.subtract, ["t1", "t2"], ["Yb"])
                P.tt("dve", t1[:], Xf[:, 0, sl], Ff[:, 1, sl], ALU.mult, ["Xb", fkey], ["t1"])
                P.tt("pool", t2[:], Xf[:, 1, sl], Ff[:, 0, sl], ALU.mult, ["Xb", fkey], ["t2"])
                P.tt("dve", Yf[:, 1, sl], t1[:], t2[:], ALU.add, ["t1", "t2"], ["Yb"])
            for c0 in range(0, 64, 4):
                pz = PS[(c0 // 4) % 2]
                for i in range(4):
                    for r_ in range(2):
                        for h in range(2):
                            P.mm(pz[HS[h], i * 128:(i + 1) * 128], Yb[HS[h], r_, c0 + i, :], I2b[HS[h], r_, :], r_ == 0, r_ == 1, ["Yb", "I2b"],
                                 ["pa%d" % ((c0 // 4) % 2)], signal=(i == 3 and r_ == 1 and h == 1), tp=TP[h])
                P.evac(Ab[:, :, :, c0:c0 + 4].rearrange("p r n c -> p (r n) c"), pz.rearrange("p (i m) -> p m i", i=4),
                       ["pa%d" % ((c0 // 4) % 2)], ["Ab"])
            for bi, (n20, nb) in enumerate([(0, 15), (15, 15), (30, 15), (45, 15), (60, 4)]):
                py = PS[4 + bi % 2]
                for i in range(nb):
                    n2 = n20 + i
                    for r_ in range(2):
                        for h in range(2):
                            P.mm(py[HS[h], i * 34:(i + 1) * 34], Ab[HS[h], r_, n2, 0:64], I1b[HS[h], n2, r_, :], r_ == 0, r_ == 1, ["Ab", "I1b"],
                                 ["py%d" % (bi % 2)], signal=(i == nb - 1 and r_ == 1 and h == 1), tp=TP[h])
                P.evac(yT[:, :, n20:n20 + nb], py[:, 0:nb * 34].rearrange("p (i j) -> p j i", j=34), ["py%d" % (bi % 2)], ["yT"])
            yflat = yT[:].rearrange("p a b -> p (a b)")
            P.tt("dve", yflat[:, 56:56 + NM], yflat[:, 56:56 + NM], ubt[:], ALU.add, ["yT", ukey_], ["yT"])
            P.tt("dve", x0t[:], yflat[:, 56:56 + NM], x0t[:], ALU.mult, ["yT", xkey_], [xkey_])
            P.dma("sp", hyT_d[jj * 128:(jj + 1) * 128, :], x0t[:], [xkey_], ["hyT_d"])
    P.barrier()

    if KSTOP == '3b':
        P.barrier(); P.emit(); es.close(); return nc
    def rstd_bufs(ph, tag):
        sq = [sbuf(ph, "nsq%s%d" % (tag, i), (128, 8, 512), BF16) for i in range(2)]
        rs = [sbuf(ph, "nrs%s%d" % (tag, i), (128, 512), F32) for i in range(2)]
        return sq, rs

    def rstd_tile(bufs, ti, c_abs, w):
        sq, rs = bufs
        i = ti % 2
        P.act(sq[i][:, :, 0:w], hT[:, :, c_abs:c_abs + w], AF.Square, ["hT"], ["nsq%d" % i])
        for k in range(8):
            P.mm(PS[6][:, 0:w], onesb[:], sq[i][:, k, 0:w], k == 0, k == 7, ["onesb", "nsq%d" % i], ["nss"])
        P.act(rs[i][:, 0:w], PS[6][:, 0:w], AF.Sqrt, ["nss"], ["nrs%d" % i], bias=1e-6, scale=1.0 / D)
        P.recip(rs[i][:, 0:w], rs[i][:, 0:w], ["nrs%d" % i], ["nrs%d" % i])
        return rs[i], "nrs%d" % i

    def norm_feat(ph, col0, tiles, Acol, Bcol, outT, okey, tag):
        bufs = rstd_bufs(ph, tag)
        tmp = [sbuf(ph, "ntm%s%d" % (tag, i), (128, 512), F32) for i in range(2)]
        for ti, (c0, w) in enumerate(tiles):
            rstd, rkey = rstd_tile(bufs, ti, col0 + c0, w)
            for k in range(8):
                t_ = tmp[k % 2]
                P.stt(t_[:, 0:w], hT[:, k, col0 + c0:col0 + c0 + w], Acol(k), rstd[:, 0:w], ALU.mult, ALU.mult, ["hT", rkey, "sc", "VT"], ["ntm%d" % (k % 2)])
                P.act(outT[:, k, c0:c0 + w], t_[:, 0:w], AF.Identity, ["ntm%d" % (k % 2), "mod"], [okey], bias=Bcol(k))

    def mlp(li, col0, tiles, tag):
        with ExitStack() as ph:
            a2 = sbuf(ph, "a2" + tag, (128, 8, NM), BF16)
            w1c = [sbuf(ph, "w1c%s%d" % (tag, i), (128, 8, 8, 128), BF16) for i in range(2)]
            w2r = [sbuf(ph, "w2r%s%d" % (tag, i), (128, 8, D), BF16) for i in range(2)]
            hid = [sbuf(ph, "hid%s%d" % (tag, i), (128, 8, 512), BF16) for i in range(2)]
            rl = [sbuf(ph, "rl%s%d" % (tag, i), (128, 512), BF16) for i in range(2)]
            cnt = 0
            def wload(g):
                gb = g % 2
                for jj in range(8):
                    j = 8 * g + jj
                    P.dma("pool", w1c[gb][:, jj], w1[li].rearrange("(k p) c -> p k c", p=128)[:, :, j * 128:(j + 1) * 128], (), ["w1c%d_%d" % (gb, jj)])
                P.dma("pool", w2r[gb][:], w2[li][g * 1024:(g + 1) * 1024, :].rearrange("(j p) c -> p j c", p=128), (), ["w2r%d" % gb])
            wload(0)
            norm_feat(ph, col0, tiles, lambda k: SCv("A2_%d" % li, k), lambda k: MODv(li, 3, k), a2, "a2", tag)
            for g in range(4):
                gb = g % 2
                if g + 1 < 4:
                    wload(g + 1)
                for ti, (c0, w) in enumerate(tiles):
                    hb = ti % 2
                    for jj in range(8):
                        pb = cnt % 2
                        cnt += 1
                        ph_ = PS[pb]
                        for k in range(8):
                            P.mm(ph_[:, 0:w], w1c[gb][:, jj, k, :], a2[:, k, c0:c0 + w], k == 0, k == 7, ["w1c%d_%d" % (gb, jj), "a2"], ["mh%d" % pb])
                        P.act(rl[pb][:, 0:w], ph_[:, 0:w], AF.Relu, ["mh%d" % pb], ["rl%d" % pb])
                        P.tt("pool" if jj % 4 == 3 else "dve", hid[hb][:, jj, 0:w], rl[pb][:, 0:w], rl[pb][:, 0:w], ALU.mult, ["rl%d" % pb], ["hid%d" % hb])
                    for m in range(8):
                        po = PS[2 + m % 4]
                        for jj in range(8):
                            P.mm(po[:, 0:w], w2r[gb][:, jj, m * 128:(m + 1) * 128], hid[hb][:, jj, 0:w], jj == 0, jj == 7, ["w2r%d" % gb, "hid%d" % hb], ["mo%d" % (m % 4)])
                        P.stt(hT[:, m, col0 + c0:col0 + c0 + w], po[:, 0:w], MODv(li, 5, m), hT[:, m, col0 + c0:col0 + c0 + w], ALU.mult, ALU.add,
                              ["mo%d" % (m % 4), "mod", "hT"], ["hT"])
        P.barrier()

    hT = sbuf(es, "hT", (128, 8, NM), F32)
    with ExitStack() as ph:
        hyT = sbuf(ph, "hyT", (128, 4, NM), BF16)
        P.dma("sp", hyT[:], hyT_d.rearrange("(j p) t -> p j t", p=128), ["hyT_d"], ["hyT"])
        attT = sbuf(ph, "attT4", (128, 4, NM), BF16)
        P.dma("sp", attT[:], attT_d, ["attT_d"], ["attT"])
        xt = [sbuf(ph, "xo%d" % i, (128, D), F32) for i in range(2)]
        for r in range((NM + 127) // 128):
            r0 = r * 128
            nr = min(128, NM - r0)
            b = r % 2
            P.dma("sp", xt[b][0:nr, :], x_m[1 + r0:1 + r0 + nr, :], (), ["xo%d" % b])
            for half in range(2):
                pt = PS[half]
                for kk in range(4):
                    k = half * 4 + kk
                    P.tr(pt[:, kk * 128:kk * 128 + nr], xt[b][0:nr, k * 128:(k + 1) * 128], ident[0:nr, 0:nr], ["xo%d" % b, "ident"], ["ps%d" % half], signal=(kk == 3))
                for kk in range(4):
                    k = half * 4 + kk
                    P.act(hT[:, k, r0:r0 + nr], pt[:, kk * 128:kk * 128 + nr], AF.Identity, ["ps%d" % half, "sc"], ["hT"], bias=SCv("GB", k))
        wo = [sbuf(ph, "wo%d" % i, (128, 8, 128), BF16) for i in range(2)]
        for m in range(8):
            wi = m % 2
            P.dma("pool", wo[wi][:], w_out.rearrange("(k p) c -> p k c", p=128)[:, :, m * 128:(m + 1) * 128], (), ["wo%d" % wi])
            for ti, (c0, w) in enumerate(MT):
                po = PS[2 + ti % 2]
                for k in range(8):
                    srcT = attT[:, k, c0:c0 + w] if k < 4 else hyT[:, k - 4, c0:c0 + w]
                    P.mm(po[:, 0:w], wo[wi][:, k, :], srcT, k == 0, k == 7, ["wo%d" % wi, "attT", "hyT"], ["po%d" % (ti % 2)])
                P.stt(hT[:, m, c0:c0 + w], po[:, 0:w], MODv(0, 2, m), hT[:, m, c0:c0 + w], ALU.mult, ALU.add, ["po%d" % (ti % 2), "mod", "hT"], ["hT"])
        if DBG:
            P.dma("sp", dbg_h[0], hT[:], ["hT"], ["dbg_h"])
    P.barrier()

    mlp(0, 0, MT, "m0")
    if DBG:
        P.dma("sp", dbg_h[1], hT[:], ["hT"], ["dbg_h"])
        P.barrier()

    MAIN = [(i * 512, 512) for i in range(4)]
    with ExitStack() as ph:
        aP = sbuf(ph, "aP", (128, 8, NM), F32)
        norm_featF = None
        pbufs = rstd_bufs(ph, "p")
        for ti, (c0, w) in enumerate(MT):
            rstd, rkey = rstd_tile(pbufs, ti, c0, w)
            for k in range(8):
                P.stt(aP[:, k, c0:c0 + w], hT[:, k, c0:c0 + w], SCv("A1_1", k), rstd[:, 0:w], ALU.mult, ALU.mult, ["hT", rkey, "sc"], ["aP%d" % k])
                P.act(aP[:, k, c0:c0 + w], aP[:, k, c0:c0 + w], AF.Identity, ["aP%d" % k, "mod"], ["aP%d" % k], bias=MODv(1, 0, k))
        mk = sbuf(ph, "mk", (128, 16), F32)
        ce = sbuf(ph, "ce", (128, 4, 16), F32)
        P.dma("sp", mk[:, 0:8], t_maskm[:, 0:8], (), ["mk"])
        P.dma("sp", mk[:, 8:16], t_maskm[:, NM - 8:NM], (), ["mk"])
        P.dma("sp", ce[:].rearrange("p a b -> p (a b)"), t_ce, (), ["ce"])
        sA = sbuf(ph, "sA", (128, NM), F32)
        sB = sbuf(ph, "sB", (128, NM), F32)
        sC, sD = sA, sB
        dl = sbuf(ph, "dl", (128, 8, 2048), BF16)
        for k in range(8):
            g = k // 2
            win = (2, 4, 8, 16)[g]
            ak = aP[:, k, :]
            pe_ = "dve"
            P.tt("pool", ak[:, 0:8], ak[:, 0:8], mk[:, 0:8], ALU.mult, ["aP%d" % k, "mk"], ["aP%d" % k])
            P.tt("pool", ak[:, NM - 8:NM], ak[:, NM - 8:NM], mk[:, 8:16], ALU.mult, ["aP%d" % k, "mk"], ["aP%d" % k])
            P.tt(pe_, (sA if pe_ == "dve" else sC)[:, 1:NM], ak[:, 0:NM - 1], ak[:, 1:NM], ALU.add, ["aP%d" % k], ["sA" if pe_ == "dve" else "sC"])
            cur, ckey, oth, okey = (sA, "sA", sB, "sB") if pe_ == "dve" else (sC, "sC", sD, "sD")
            lo, hi = 1, NM
            stepw = 1
            while stepw * 2 < win:
                nlo, nhi = lo + stepw, hi - stepw
                P.tt(pe_, oth[:, nlo:nhi], cur[:, nlo - stepw:nhi - stepw], cur[:, nlo + stepw:nhi + stepw], ALU.add, [ckey], [okey])
                cur, ckey, oth, okey = oth, okey, cur, ckey
                lo, hi = nlo, nhi
                stepw *= 2
            P.ts(pe_, oth[:, 8:2056], cur[:, 8:2056], 1.0 / win, None, ALU.mult, None, [ckey], [okey])
            P.tt(pe_, oth[:, 8:16], cur[:, 8:16], ce[:, g, 0:8], ALU.mult, [ckey, "ce", okey], [okey])
            P.tt(pe_, oth[:, 2048:2056], cur[:, 2048:2056], ce[:, g, 8:16], ALU.mult, [ckey, "ce", okey], [okey])
            P.tt(pe_, dl[:, k, :], oth[:, 8:2056], ak[:, 8:2056], ALU.subtract, [okey, "aP%d" % k], ["dl"])
        pwt = [sbuf(ph, "pwt%d" % i, (128, 2, 256), BF16) for i in range(2)]
        for g in range(4):
            P.dma("pool", pwt[g % 2][:], pool_w[g].rearrange("(k p) c -> p k c", p=128), (), ["pwt%d" % (g % 2)])
            for m2 in range(2):
                m = 2 * g + m2
                for ti, (c0, w) in enumerate(MAIN):
                    po = PS[2 + ti % 2]
                    for k2 in range(2):
                        P.mm(po[:, 0:w], pwt[g % 2][:, k2, m2 * 128:(m2 + 1) * 128], dl[:, 2 * g + k2, c0:c0 + w], k2 == 0, k2 == 1, ["pwt%d" % (g % 2), "dl"], ["po%d" % (ti % 2)])
                    P.stt(hT[:, m, 8 + c0:8 + c0 + w], po[:, 0:w], SCv("GS", m), hT[:, m, 8 + c0:8 + c0 + w], ALU.mult, ALU.add, ["po%d" % (ti % 2), "sc", "hT"], ["hT"])
        if DBG:
            P.dma("sp", dbg_h[2], hT[:], ["hT"], ["dbg_h"])
    P.barrier()

    mlp(1, 8, MAIN, "m1")

    with ExitStack() as ph:
        fbufs = rstd_bufs(ph, "f")
        of = sbuf(ph, "of", (128, 8, 512), F32)
        ot = [sbuf(ph, "ot%d" % i, (128, D), F32) for i in range(2)]
        oc = 0
        for ti, (c0, w) in enumerate(MAIN):
            rstd, rkey = rstd_tile(fbufs, ti, 8 + c0, w)
            for k in range(8):
                P.stt(of[:, k, 0:w], hT[:, k, 8 + c0:8 + c0 + w], V("fing", k), rstd[:, 0:w], ALU.mult, ALU.mult, ["hT", rkey, "VT"], ["of"])
            for tcn in range(4):
                b = oc % 2
                oc += 1
                for half in range(2):
                    pt = PS[half]
                    for kk in range(4):
                        k = half * 4 + kk
                        P.tr(pt[:, kk * 128:(kk + 1) * 128], of[:, k, tcn * 128:(tcn + 1) * 128], ident[:], ["of", "ident"], ["ps%d" % half], signal=(kk == 3))
                    P.evac(ot[b][:, half * 512:(half + 1) * 512], pt[:], ["ps%d" % half], ["ot%d" % b])
                P.dma("sp", out_d[c0 + tcn * 128:c0 + (tcn + 1) * 128, :], ot[b][:], ["ot%d" % b], ["out_d"])
    P.barrier()
    P.emit()
    es.close()
    return nc


_CACHE = {}


def _core_inputs(inp, b, hf):
    f = lambda a: np.ascontiguousarray(a, dtype=np.float32)
    m0 = 2048 * hf
    xb = inp["x"][b]
    xm = np.zeros((NMX, D), np.float32)
    lo, hi = m0 - 9, m0 + 2057
    slo, shi = max(lo, 0), min(hi, L)
    xm[slo - lo:shi - lo] = xb[slo:shi]
    D1, F2, I2, I1 = _fft_tables(hf)
    cos_a, sin_a = _rope_tables(np.arange(L))
    cos_m, sin_m = _rope_tables(np.clip(np.arange(m0 - 8, m0 + 2056), 0, L - 1))
    zT, trow, negd = _hyena_consts()
    maskm, ce, maskx = _pool_consts(hf)
    d = {
        "x_all": f(xb), "x_m": xm, "ctx": f(inp["ctx"][b]),
        "cvec": f(inp["c"][b].reshape(8, 128)), "cctx": f(inp["c_ctx"].reshape(8, 128)),
        "ada_w": f(inp["ada_w"]), "ada_b": f(inp["ada_b"].reshape(2, 48, 128)),
        "norm1_g": f(inp["norm1_g"].reshape(16, 128)), "norm2_g": f(inp["norm2_g"].reshape(16, 128)),
        "mix_w_in": f(inp["mix_w_in"][0]), "mix_b_in": f(inp["mix_b_in"][0].reshape(24, 128)), "b_in_row": f(inp["mix_b_in"][0].reshape(1, 3072)),
        "mix_w_out": f(inp["mix_w_out"][0]), "mix_b_out": f(inp["mix_b_out"][0].reshape(8, 128)),
        "lamv": f(np.stack([inp["lam_q1"][0], inp["lam_k1"][0], inp["lam_q2"][0], inp["lam_k2"][0]]).reshape(1, 256)),
        "subln_g": f(inp["subln_g"].reshape(1, 128)),
        "hy_conv_w": f(inp["hy_conv_w"][0].reshape(36, 128)), "hy_conv_b": f(inp["hy_conv_b"][0].reshape(12, 128)),
        "hy_pos_w1": f(inp["hy_pos_w1"][0]),
        "hy_pvec": f(np.stack([inp["hy_pos_b1"][0], inp["hy_freq1"][0], inp["hy_pos_b2"][0], inp["hy_freq2"][0]])),
        "hy_pos_w2": f(inp["hy_pos_w2"][0]), "hy_pos_w3": f(inp["hy_pos_w3"][0]),
        "hy_bias": f(inp["hy_bias"][0].reshape(4, 128)),
        "pool_w": f(inp["pool_w"][0]), "pool_scale": f(inp["pool_scale"][0].reshape(8, 128)),
        "mlp_w1": f(inp["mlp_w1"]), "mlp_w2": f(inp["mlp_w2"]), "final_g": f(inp["final_g"].reshape(8, 128)),
        "t_ident": np.eye(128, dtype=np.float32), "t_perm": _perm_matrix(),
        "t_cos_a": cos_a, "t_sin_a": sin_a, "t_cos_m": cos_m, "t_sin_m": sin_m,
        "t_zT": zT, "t_trow": trow, "t_negd": negd,
        "t_D1": D1, "t_F2": F2, "t_I2": I2, "t_I1": I1,
        "t_maskm": maskm, "t_ce": f(ce.reshape(128, 64)), "t_maskx": maskx,
    }
    return d


def kernel(**inputs):
    inp = {k: np.asarray(v) for k, v in inputs.items()}
    if "nc" not in _CACHE:
        _CACHE["nc"] = build_program()
    nc = _CACHE["nc"]
    in_maps = []
    for core in range(8):
        b, hf = core // 2, core % 2
        in_maps.append(_core_inputs(inp, b, hf))
    res = run_bass_kernel_spmd(nc, in_maps, core_ids=list(range(8)))
    _CACHE["res"] = res
    out = np.zeros((4, L, D), np.float32)
    for core in range(8):
        b, hf = core // 2, core % 2
        out[b, 2048 * hf:2048 * (hf + 1)] = res.results[core]["out"]
    return out
```
